# Optimizing a Trainium2 kernel written in Bass

```python
import math
import jax
import jax.numpy as jnp
from jax import lax
import numpy as np

D_MODEL = 2048
BATCH = 8
SEQ = 2048
DEPTH = 2

GRID_W = 64
CTX_LEN = 256
HEAD_DIM = 128
N_MIXERS = 4
GROUP_HEADS = D_MODEL // HEAD_DIM // N_MIXERS
GROUP_WIDTH = GROUP_HEADS * HEAD_DIM
MIX_WIDTH = N_MIXERS * GROUP_WIDTH
GQA_HEADS = GROUP_HEADS
GQA_KV_HEADS = GROUP_HEADS // 2
NA_HEADS = GROUP_HEADS
WIN_ROWS = 8
WIN_COLS = 16
DIFF_HEADS = GROUP_HEADS
DIFF_QK_DIM = HEAD_DIM // 2
SWA_HEADS = GROUP_HEADS
SWA_KV_HEADS = GROUP_HEADS // 2
WINDOW = 128
Q_BLOCK = 128
MLP_HIDDEN = 4 * D_MODEL
N_MOD = 6
ROPE_THETA = 10000.0
NORM_EPS = 1e-6
NEG_INF = -1e30
PROJ_HEADS = (GQA_HEADS, GQA_KV_HEADS, GQA_KV_HEADS, NA_HEADS, NA_HEADS, NA_HEADS, 2 * DIFF_HEADS, 2 * DIFF_HEADS, DIFF_HEADS, SWA_HEADS, SWA_KV_HEADS, SWA_KV_HEADS)
PROJ_DIMS = (HEAD_DIM, HEAD_DIM, HEAD_DIM, HEAD_DIM, HEAD_DIM, HEAD_DIM, DIFF_QK_DIM, DIFF_QK_DIM, HEAD_DIM, HEAD_DIM, HEAD_DIM, HEAD_DIM)
PROJ_WIDTH = sum(h * d for h, d in zip(PROJ_HEADS, PROJ_DIMS))

kernel_name = 'hybrid_parallel_group_flow_block'


def rms_norm(x, g):
    xf = x.astype(jnp.float32)
    y = xf * lax.rsqrt(jnp.mean(xf * xf, axis=-1, keepdims=True) + NORM_EPS)
    return (y * g.astype(jnp.float32)).astype(x.dtype)


def modulate(h, shift, scale):
    return h * (1 + scale) + shift


def axial_rope(n_tok, dim):
    t = jnp.arange(n_tok)
    row = (t // GRID_W).astype(jnp.float32)
    col = (t % GRID_W).astype(jnp.float32)
    n_freq = dim // 4
    inv_freq = ROPE_THETA ** (-jnp.arange(n_freq, dtype=jnp.float32) / n_freq)
    ang = jnp.concatenate([row[:, None] * inv_freq, col[:, None] * inv_freq], axis=-1)
    return jnp.cos(ang), jnp.sin(ang)


def apply_rope(x, cos, sin):
    xf = x.astype(jnp.float32).reshape(*x.shape[:-1], -1, 2)
    x0, x1 = xf[..., 0], xf[..., 1]
    c = cos[None, :, None, :]
    s = sin[None, :, None, :]
    return jnp.stack([x0 * c - x1 * s, x0 * s + x1 * c], axis=-1).reshape(x.shape).astype(x.dtype)


def split_heads(p):
    parts, start = [], 0
    for h, d in zip(PROJ_HEADS, PROJ_DIMS):
        parts.append(p[..., start:start + h * d].reshape(*p.shape[:-1], h, d))
        start += h * d
    return parts


def to_blocks(a, size):
    b, s = a.shape[:2]
    return jnp.swapaxes(a.reshape(b, s // size, size, *a.shape[2:]), 0, 1)


def from_blocks(a):
    nb, b, size = a.shape[:3]
    return jnp.swapaxes(a, 0, 1).reshape(b, nb * size, *a.shape[3:])


def gqa_attend(q, k, v, bias=None, sink=None):
    b, nq, h, d = q.shape
    hkv = k.shape[2]
    g = h // hkv
    qg = q.reshape(b, nq, hkv, g, d)
    s = jnp.einsum('bqhgd,bkhd->bhgqk', qg, k).astype(jnp.float32) * (d ** -0.5)
    if bias is not None:
        s = s + bias
    if sink is not None:
        sk = jnp.broadcast_to(sink.astype(jnp.float32).reshape(1, hkv, g, 1, 1), s.shape[:-1] + (1,))
        p = jax.nn.softmax(jnp.concatenate([s, sk], axis=-1), axis=-1)[..., :-1]
    else:
        p = jax.nn.softmax(s, axis=-1)
    o = jnp.einsum('bhgqk,bkhd->bqhgd', p.astype(v.dtype), v)
    return o.reshape(b, nq, h * v.shape[-1])


def mixer_global_gqa(q, k, v, qc, kc, vc, gq, gk, cos, sin, with_ctx):
    q = apply_rope(rms_norm(q, gq), cos, sin)
    k = apply_rope(rms_norm(k, gk), cos, sin)
    kc = rms_norm(kc, gk)
    k_all = jnp.concatenate([k, kc], axis=1)
    v_all = jnp.concatenate([v, vc], axis=1)
    out = from_blocks(lax.map(lambda qb: gqa_attend(qb, k_all, v_all), to_blocks(q, Q_BLOCK)))
    out_c = gqa_attend(rms_norm(qc, gq), kc, vc) if with_ctx else None
    return out, out_c


def mixer_neighbourhood(q, k, v, qc, kc, vc, rpb, with_ctx):
    b, s, h, d = q.shape
    rows = s // GRID_W
    wr = min(WIN_ROWS, rows)
    wc = min(WIN_COLS, GRID_W)
    n_nb = wr * wc
    scale = d ** -0.5
    k_grid = k.reshape(b, rows, GRID_W, h, d)
    v_grid = v.reshape(b, rows, GRID_W, h, d)
    col = jnp.arange(GRID_W)
    col_idx = jnp.clip(col - wc // 2, 0, GRID_W - wc)[:, None] + jnp.arange(wc)[None, :]
    dcol = col_idx - col[:, None] + WIN_COLS - 1

    def row_block(args):
        r, qr = args
        r_start = jnp.clip(r - wr // 2, 0, rows - wr)
        kr = lax.dynamic_slice_in_dim(k_grid, r_start, wr, axis=1)[:, :, col_idx]
        vr = lax.dynamic_slice_in_dim(v_grid, r_start, wr, axis=1)[:, :, col_idx]
        drow = r_start + jnp.arange(wr) - r + WIN_ROWS - 1
        bias = rpb[:, drow[None, :, None], dcol[:, None, :]].astype(jnp.float32)
        s_nb = jnp.einsum('bqhd,biqjhd->bhqij', qr, kr).astype(jnp.float32) * scale + bias[None]
        s_cx = jnp.einsum('bqhd,bkhd->bhqk', qr, kc).astype(jnp.float32) * scale
        p = jax.nn.softmax(jnp.concatenate([s_nb.reshape(b, h, GRID_W, n_nb), s_cx], axis=-1), axis=-1).astype(v.dtype)
        o = (jnp.einsum('bhqij,biqjhd->bqhd', p[..., :n_nb].reshape(b, h, GRID_W, wr, wc), vr)
             + jnp.einsum('bhqk,bkhd->bqhd', p[..., n_nb:], vc))
        return o.reshape(b, GRID_W, h * d)

    out = from_blocks(lax.map(row_block, (jnp.arange(rows), to_blocks(q, GRID_W))))
    out_c = gqa_attend(qc, kc, vc) if with_ctx else None
    return out, out_c


def diff_attend(q, k, v, lam):
    s = jnp.einsum('bqhd,bkhd->bhqk', q, k).astype(jnp.float32) * (q.shape[-1] ** -0.5)
    a = jax.nn.softmax(s, axis=-1)
    p = (a[:, 0::2] - lam * a[:, 1::2]).astype(v.dtype)
    return jnp.einsum('bhqk,bkhd->bqhd', p, v)


def mixer_diff(q, k, v, qc, kc, vc, lq1, lk1, lq2, lk2, gsub, lam_init, cos, sin, with_ctx):
    f32 = jnp.float32
    lam = (jnp.exp(jnp.sum(lq1.astype(f32) * lk1.astype(f32)))
           - jnp.exp(jnp.sum(lq2.astype(f32) * lk2.astype(f32))) + lam_init)
    q = apply_rope(q, cos, sin)
    k = apply_rope(k, cos, sin)
    k_all = jnp.concatenate([k, kc], axis=1)
    v_all = jnp.concatenate([v, vc], axis=1)

    def post(o):
        return (rms_norm(o, gsub) * (1 - lam_init)).reshape(o.shape[0], o.shape[1], -1)

    out = from_blocks(lax.map(lambda qb: post(diff_attend(qb, k_all, v_all, lam)), to_blocks(q, Q_BLOCK)))
    out_c = post(diff_attend(qc, kc, vc, lam)) if with_ctx else None
    return out, out_c


def mixer_window_gqa(q, k, v, qc, kc, vc, sink, cos, sin, with_ctx):
    b, s = q.shape[:2]
    span = Q_BLOCK + 2 * WINDOW
    q = apply_rope(q, cos, sin)
    k = apply_rope(k, cos, sin)
    pad = ((0, 0), (WINDOW, WINDOW), (0, 0), (0, 0))
    k_pad = jnp.pad(k, pad)
    v_pad = jnp.pad(v, pad)
    ctx_bias = jnp.zeros((Q_BLOCK, kc.shape[1]), jnp.float32)

    def block(args):
        i, qb = args
        start = i * Q_BLOCK
        kb = lax.dynamic_slice_in_dim(k_pad, start, span, axis=1)
        vb = lax.dynamic_slice_in_dim(v_pad, start, span, axis=1)
        q_pos = start + jnp.arange(Q_BLOCK)
        k_pos = start - WINDOW + jnp.arange(span)
        valid = (k_pos[None, :] >= 0) & (k_pos[None, :] < s) & (jnp.abs(q_pos[:, None] - k_pos[None, :]) <= WINDOW)
        bias = jnp.concatenate([jnp.where(valid, 0.0, NEG_INF), ctx_bias], axis=-1)
        return gqa_attend(qb, jnp.concatenate([kb, kc], axis=1), jnp.concatenate([vb, vc], axis=1), bias=bias, sink=sink)

    out = from_blocks(lax.map(block, (jnp.arange(s // Q_BLOCK), to_blocks(q, Q_BLOCK))))
    out_c = gqa_attend(qc, kc, vc, sink=sink) if with_ctx else None
    return out, out_c


def squared_relu_mlp(h, w_up, w_down):
    return jnp.square(jax.nn.relu(h @ w_up)) @ w_down


def setup_inputs(seed: int = 0) -> dict:
    key = jax.random.key(seed)
    ks = iter(jax.random.split(key, 24))

    def nrm(shape, scale=1.0):
        return jax.random.normal(next(ks), shape, jnp.float32) * scale

    def gain(shape):
        return 1.0 + 0.1 * nrm(shape)

    L = DEPTH
    return {
        'x': nrm((BATCH, SEQ, D_MODEL)),
        'c': nrm((BATCH, D_MODEL)),
        'ctx': nrm((BATCH, CTX_LEN, D_MODEL)),
        'c_ctx': nrm((D_MODEL,)),
        'g_mix': gain((L, D_MODEL)),
        'g_mlp': gain((L, D_MODEL)),
        'w_mod': nrm((L, D_MODEL, N_MOD * D_MODEL), 0.5 * D_MODEL ** -0.5),
        'b_mod': nrm((L, N_MOD * D_MODEL), 0.02),
        'w_in': nrm((L, D_MODEL, PROJ_WIDTH), D_MODEL ** -0.5),
        'w_out': nrm((L, MIX_WIDTH, D_MODEL), MIX_WIDTH ** -0.5),
        'gqa_gq': gain((L, HEAD_DIM)),
        'gqa_gk': gain((L, HEAD_DIM)),
        'na_rpb': nrm((L, NA_HEADS, 2 * WIN_ROWS - 1, 2 * WIN_COLS - 1), 0.1),
        'diff_lq1': nrm((L, DIFF_QK_DIM), 0.1),
        'diff_lk1': nrm((L, DIFF_QK_DIM), 0.1),
        'diff_lq2': nrm((L, DIFF_QK_DIM), 0.1),
        'diff_lk2': nrm((L, DIFF_QK_DIM), 0.1),
        'diff_gsub': gain((L, HEAD_DIM)),
        'swa_sink': nrm((L, SWA_HEADS), 0.5),
        'w_up': nrm((L, D_MODEL, MLP_HIDDEN), D_MODEL ** -0.5),
        'w_down': nrm((L, MLP_HIDDEN, D_MODEL), MLP_HIDDEN ** -0.5),
        'g_final': gain((D_MODEL,)),
    }


def reference(x, c, ctx, c_ctx, g_mix, g_mlp, w_mod, b_mod, w_in, w_out, gqa_gq, gqa_gk, na_rpb,
              diff_lq1, diff_lk1, diff_lq2, diff_lk2, diff_gsub, swa_sink, w_up, w_down, g_final):
    b, s, _ = x.shape
    cos_h, sin_h = axial_rope(s, HEAD_DIM)
    cos_d, sin_d = axial_rope(s, DIFF_QK_DIM)
    cond_x = jax.nn.silu(c)
    cond_c = jax.nn.silu(c_ctx)
    for l in range(DEPTH):
        with_ctx = l < DEPTH - 1
        mod_x = (cond_x @ w_mod[l] + b_mod[l]).reshape(b, N_MOD, 1, D_MODEL)
        mod_c = (cond_c @ w_mod[l] + b_mod[l]).reshape(N_MOD, 1, D_MODEL)
        hx = modulate(rms_norm(x, g_mix[l]), mod_x[:, 0], mod_x[:, 1])
        hc = modulate(rms_norm(ctx, g_mix[l]), mod_c[0], mod_c[1])
        (aq, ak, av, bq, bk, bv, cq, ck, cv, dq, dk, dv) = split_heads(hx @ w_in[l])
        (aqc, akc, avc, bqc, bkc, bvc, cqc, ckc, cvc, dqc, dkc, dvc) = split_heads(hc @ w_in[l])
        oa, oa_c = mixer_global_gqa(aq, ak, av, aqc, akc, avc, gqa_gq[l], gqa_gk[l], cos_h, sin_h, with_ctx)
        ob, ob_c = mixer_neighbourhood(bq, bk, bv, bqc, bkc, bvc, na_rpb[l], with_ctx)
        oc, oc_c = mixer_diff(cq, ck, cv, cqc, ckc, cvc, diff_lq1[l], diff_lk1[l], diff_lq2[l], diff_lk2[l],
                              diff_gsub[l], 0.8 - 0.6 * math.exp(-0.3 * l), cos_d, sin_d, with_ctx)
        od, od_c = mixer_window_gqa(dq, dk, dv, dqc, dkc, dvc, swa_sink[l], cos_h, sin_h, with_ctx)
        x = x + mod_x[:, 2] * (jnp.concatenate([oa, ob, oc, od], axis=-1) @ w_out[l])
        x = x + mod_x[:, 5] * squared_relu_mlp(modulate(rms_norm(x, g_mlp[l]), mod_x[:, 3], mod_x[:, 4]), w_up[l], w_down[l])
        if with_ctx:
            ctx = ctx + mod_c[2] * (jnp.concatenate([oa_c, ob_c, oc_c, od_c], axis=-1) @ w_out[l])
            ctx = ctx + mod_c[5] * squared_relu_mlp(modulate(rms_norm(ctx, g_mlp[l]), mod_c[3], mod_c[4]), w_up[l], w_down[l])
    return rms_norm(x, g_final)
```

```python
import math
from contextlib import ExitStack
import numpy as np
import concourse.bass as bass
import concourse.mybir as mybir
from concourse.bass_utils import run_bass_kernel_spmd

F32 = mybir.dt.float32
BF16 = mybir.dt.bfloat16
AF = mybir.ActivationFunctionType
ALU = mybir.AluOpType
AX = mybir.AxisListType

D = 2048
SX = 2048
SC = 256
NT = SX + SC
L = 2
KC = 16
HID = 8192
PROJ = 5120
NEG = -30000.0
EPS = 1e-6
NB_TILES = {0: list(range(0, 6)), 1: list(range(2, 10)), 2: list(range(6, 14)), 3: list(range(10, 16))}
NB_OFF = {0: 0, 1: 6, 2: 14, 3: 22}
NB_NT = 28

_SM = {}
_off = 0


def _reg(name, w):
    global _off
    _SM[name] = (_off, w)
    _off += w


_reg("cond", 32)
for _l in range(L):
    _reg("gmix%d" % _l, 16)
    _reg("gmlp%d" % _l, 16)
    _reg("bmod%d" % _l, 96)
    _reg("gq%d" % _l, 1)
    _reg("gk%d" % _l, 1)
    _reg("gsub%d" % _l, 1)
    for _n in ("lq1", "lk1", "lq2", "lk2"):
        _reg("%s%d" % (_n, _l), 64)
    _reg("sink%d" % _l, 4)
_reg("gfin", 16)
NS = _off


def lam_init(l):
    return 0.8 - 0.6 * math.exp(-0.3 * l)


class Buf:
    __slots__ = ("w", "rs")

    def __init__(self):
        self.w = None
        self.rs = {}


class Sched:
    R = 8

    def __init__(self, nc):
        self.nc = nc
        self.engs = {"pe": nc.tensor, "act": nc.scalar, "dve": nc.vector, "pool": nc.gpsimd, "sp": nc.sync}
        self.sem = {e: nc.alloc_semaphore("sem_" + e) for e in ("pe", "act", "dve", "pool")}
        self.cnt = {e: 0 for e in self.sem}
        self.seen = {e: {} for e in self.engs}
        self.dsem = {q: [nc.alloc_semaphore("dsem_%s_%d" % (q, i)) for i in range(self.R)] for q in ("sp", "pool")}
        self.dcnt = {q: [0] * self.R for q in self.dsem}
        self.drr = {q: 0 for q in self.dsem}

    def _wait(self, eng, tok):
        if tok is None:
            return
        sem, key, val = tok
        if key == "pe" and eng == "pe":
            return
        if self.seen[eng].get(key, 0) >= val:
            return
        self.engs[eng].wait_ge(sem, val)
        self.seen[eng][key] = val

    def _deps(self, eng, r, w):
        for b in r:
            self._wait(eng, b.w)
        for b in w:
            self._wait(eng, b.w)
            for key, (sem, val) in b.rs.items():
                self._wait(eng, (sem, key, val))

    def _commit(self, tok, r, w):
        sem, key, val = tok
        for b in r:
            b.rs[key] = (sem, val)
        for b in w:
            b.w = tok
            b.rs = {}

    def op(self, eng, fn, r=(), w=()):
        self._deps(eng, r, w)
        ins = fn()
        self.cnt[eng] += 1
        ins.then_inc(self.sem[eng], 1)
        self._commit((self.sem[eng], eng, self.cnt[eng]), r, w)

    def dma(self, q, out, in_, r=(), w=()):
        i = self.drr[q]
        self.drr[q] = (i + 1) % self.R
        sem = self.dsem[q][i]
        key = (q, i)
        if self.dcnt[q][i] > 0:
            self._wait(q, (sem, key, 16 * self.dcnt[q][i]))
        self._deps(q, r, w)
        self.engs[q].dma_start(out=out, in_=in_).then_inc(sem, 16)
        self.dcnt[q][i] += 1
        self._commit((sem, key, 16 * self.dcnt[q][i]), r, w)

    def barrier(self):
        for e in self.engs:
            for c in self.sem:
                if self.cnt[c] > 0:
                    self._wait(e, (self.sem[c], c, self.cnt[c]))
            for q in self.dsem:
                for i in range(self.R):
                    if self.dcnt[q][i] > 0:
                        self._wait(e, (self.dsem[q][i], (q, i), 16 * self.dcnt[q][i]))


def chunks(t0, t1, step):
    out = []
    t = t0
    while t < t1:
        n = min(step, t1 - t)
        out.append((t, n))
        t += n
    return out


def tok_chunks(t_end, step):
    res = [(t, n, 0) for (t, n) in chunks(0, min(t_end, SX), step)]
    if t_end > SX:
        res += [(t, n, 1) for (t, n) in chunks(SX, t_end, step)]
    return res


def build(debug=False, stop_after=None):
    nc = bass.Bass("TRN2", target_bir_lowering=False)
    S = Sched(nc)
    es = ExitStack()

    def din(name, shape, dt=F32):
        return nc.dram_tensor(name, shape, dt, kind="ExternalInput").ap()

    def dscr(name, shape, dt):
        if debug:
            return nc.dram_tensor(name, shape, dt, kind="ExternalOutput").ap()
        return nc.dram_tensor(name, shape, dt).ap()

    xin = din("xin", [KC, 128, NT])
    small_d = din("small", [128, NS])
    rope_d = din("rope", [128, 4, NT])
    consts_d = din("consts", [128, 256])
    dmask_d = din("dmask", [128, 6, 512])
    wmod_d = din("wmod", [L, D, 6 * D])
    win_d = din("win", [L, D, PROJ])
    wout_d = din("wout", [L, D, D])
    wup_d = din("wup", [L, D, HID])
    wdown_d = din("wdown", [L, HID, D])
    nbias_d = din("nbias", [L, 4, NB_NT, 128, 512])
    out_d = nc.dram_tensor("out", [KC, 128, SX], F32, kind="ExternalOutput").ap()

    xres = dscr("xres", [KC, 128, NT], F32)
    hT_d = dscr("hT", [KC, 128, NT], BF16)
    qT_d = dscr("qT", [16, 128, NT], BF16)
    kT_d = dscr("kT", [12, 128, NT], BF16)
    v_d = dscr("vS", [18, 128, 1536], BF16)
    oT_d = dscr("oT", [KC, 128, NT], BF16)

    uid = [0]

    def sb(name, shape, dt, stack=None):
        uid[0] += 1
        return (stack or es).enter_context(nc.sbuf_tensor("s%d_%s" % (uid[0], name), shape, dt))

    small = sb("small", [128, NS], F32)
    b_small = Buf()
    cst = sb("cst", [128, 256], BF16)
    b_cst = Buf()
    ident = cst[:, 0:128]
    ones = cst[:, 128:256]
    modsb = sb("modsb", [128, L, 96, 2], F32)
    b_mod = Buf()
    der = sb("der", [128, L, 80], F32)
    b_der = Buf()
    condT = sb("condT", [128, 32], BF16)
    b_cond = Buf()
    ring = [sb("ring%d" % i, [128, 8192], BF16) for i in range(3)]
    b_ring = [Buf() for _ in range(3)]
    ring_ctr = [0]
    bank2 = [nc.alloc_psum_tensor("bank2_%d" % i, [128, 2, 512], F32) for i in range(4)]
    banks = [bank2[i // 2][:, i % 2, :] for i in range(8)]
    b_bank = [Buf() for _ in range(8)]

    def smv(name):
        o, w = _SM[name]
        return small[:, o:o + w]

    def ACT(out, in_, func, r, w, **kw):
        S.op("act", lambda: nc.scalar.activation(out=out, in_=in_, func=func, **kw), r=r, w=w)

    def TT(eng, out, in0, in1, op, r, w):
        e = nc.vector if eng == "dve" else nc.gpsimd
        S.op(eng, lambda: e.tensor_tensor(out=out, in0=in0, in1=in1, op=op), r=r, w=w)

    def STT(out, in0, scalar, in1, op0, op1, r, w):
        S.op("dve", lambda: nc.vector.scalar_tensor_tensor(out=out, in0=in0, scalar=scalar, in1=in1, op0=op0, op1=op1), r=r, w=w)

    def TS(out, in0, s1, s2, op0, op1, r, w, eng="dve"):
        e = nc.vector if eng == "dve" else nc.gpsimd
        if op1 is None:
            S.op(eng, lambda: e.tensor_scalar(out=out, in0=in0, scalar1=s1, scalar2=None, op0=op0), r=r, w=w)
        else:
            S.op(eng, lambda: e.tensor_scalar(out=out, in0=in0, scalar1=s1, scalar2=s2, op0=op0, op1=op1), r=r, w=w)

    def MMG(fn, r, w):
        S.op("pe", fn, r=r, w=w)

    class WStream:
        def __init__(self, loads, a):
            self.loads = loads
            self.a = a
            self.issued = 0
            self.slots = {}

        def _issue(self, i):
            s = ring_ctr[0] % 3
            ring_ctr[0] += 1
            dst = ring[s][:, :].rearrange("p (a b) -> p a b", a=self.a)
            S.dma("pool", dst, self.loads[i], w=[b_ring[s]])
            self.slots[i] = s

        def get(self, i, ahead=2):
            while self.issued < min(len(self.loads), i + 1 + ahead):
                self._issue(self.issued)
                self.issued += 1
            s = self.slots[i]
            return ring[s], b_ring[s]

    S.dma("sp", small[:, :], small_d[:, :], w=[b_small])
    S.dma("pool", cst[:, :], consts_d[:, :], w=[b_cst])
    o_c, _ = _SM["cond"]
    ACT(condT[:, :], small[:, o_c:o_c + 32], AF.Silu, r=[b_small], w=[b_cond])

    b_modl = [Buf() for _ in range(L)]
    b_derl = [Buf() for _ in range(L)]
    lamt = sb("lamt", [128, 64], F32)
    b_lt = Buf()

    def mod_steps(l, t_lo, t_hi):
        loads = [wmod_d[l].rearrange("(k p) n -> p k n", p=128)[:, :, nt * 512:(nt + 1) * 512] for nt in range(t_lo, t_hi)]
        ws = WStream(loads, 16)
        pb = banks[7]
        for nt in range(t_lo, t_hi):
            wt, bw = ws.get(nt - t_lo)

            def grp():
                ins = None
                for j in range(4):
                    for kc in range(KC):
                        ins = nc.tensor.matmul(pb[:, j * 2:j * 2 + 2], lhsT=wt[:, kc * 512 + j * 128: kc * 512 + (j + 1) * 128],
                                               rhs=condT[:, kc * 2:kc * 2 + 2], start=(kc == 0), stop=(kc == KC - 1))
                return ins
            MMG(grp, r=[bw, b_cond], w=[b_bank[7]])
            pv = pb[:, 0:8].rearrange("p (n c) -> p n c", c=2)
            bm = smv("bmod%d" % l)
            for cls in range(2):
                TT("dve", modsb[:, l, nt * 4:(nt + 1) * 4, cls], pv[:, :, cls], bm[:, nt * 4:(nt + 1) * 4], ALU.add,
                   r=[b_bank[7], b_small], w=[b_modl[l]])
            yield

    def der_a(l):
        for cls in range(2):
            a1 = der[:, l, 0:32].rearrange("p (k c) -> p k c", c=2)[:, :, cls]
            STT(a1, modsb[:, l, 16:32, cls], 1.0, smv("gmix%d" % l), ALU.add, ALU.mult, r=[b_modl[l], b_small], w=[b_derl[l]])
        TS(der[:, l, 64:65], smv("gq%d" % l), 128.0 ** -0.5, None, ALU.mult, None, r=[b_small], w=[b_derl[l]])
        ACT(der[:, l, 66:70], smv("sink%d" % l), AF.Exp, r=[b_small], w=[b_derl[l]])
        TT("dve", lamt[:, :], smv("lq1%d" % l), smv("lk1%d" % l), ALU.mult, r=[b_small], w=[b_lt])
        S.op("dve", lambda: nc.vector.tensor_reduce(out=der[:, l, 72:73], in_=lamt[:, :], axis=AX.X, op=ALU.add), r=[b_lt], w=[b_derl[l]])
        TT("dve", lamt[:, :], smv("lq2%d" % l), smv("lk2%d" % l), ALU.mult, r=[b_small], w=[b_lt])
        S.op("dve", lambda: nc.vector.tensor_reduce(out=der[:, l, 73:74], in_=lamt[:, :], axis=AX.X, op=ALU.add), r=[b_lt], w=[b_derl[l]])
        ACT(der[:, l, 72:74], der[:, l, 72:74], AF.Exp, r=[b_derl[l]], w=[b_derl[l]])
        TT("dve", der[:, l, 74:75], der[:, l, 73:74], der[:, l, 72:73], ALU.subtract, r=[b_derl[l]], w=[b_derl[l]])
        TS(der[:, l, 70:71], der[:, l, 74:75], -lam_init(l), None, ALU.add, None, r=[b_derl[l]], w=[b_derl[l]])
        TS(der[:, l, 71:72], smv("gsub%d" % l), 1.0 - lam_init(l), None, ALU.mult, None, r=[b_small], w=[b_derl[l]])

    def der_b(l):
        for cls in range(2):
            a2 = der[:, l, 32:64].rearrange("p (k c) -> p k c", c=2)[:, :, cls]
            STT(a2, modsb[:, l, 64:80, cls], 1.0, smv("gmlp%d" % l), ALU.add, ALU.mult, r=[b_modl[l], b_small], w=[b_derl[l]])

    def deferred_steps():
        yield from mod_steps(0, 8, 24)
        der_b(0)
        yield
        yield from mod_steps(1, 0, 24)
        der_a(1)
        der_b(1)
        yield

    def A1(l, kc, cls):
        return der[:, l, kc * 2 + cls: kc * 2 + cls + 1]

    def A2(l, kc, cls):
        return der[:, l, 32 + kc * 2 + cls: 32 + kc * 2 + cls + 1]

    def MOD(l, j, kc, cls):
        return modsb[:, l, j * 16 + kc, cls:cls + 1]

    def norm_mod(x_c, b_x, n, sq, b_sq, rstd, b_rstd, tmp, b_tmp, bank_i, a_fn, b_fn, out_fn, b_out, final=False, rb=()):
        ACT(sq[:, :, 0:n], x_c[:, :, 0:n], AF.Square, r=[b_x], w=[b_sq])
        pb = banks[bank_i]
        b_out2 = Buf()

        def grp():
            ins = None
            for kc in range(KC):
                ins = nc.tensor.matmul(pb[:, 0:n], lhsT=ones, rhs=sq[:, kc, 0:n], start=(kc == 0), stop=(kc == KC - 1))
            return ins
        MMG(grp, r=[b_sq, b_cst], w=[b_bank[bank_i]])
        ACT(rstd[:, 0:n], pb[:, 0:n], AF.Ln, r=[b_bank[bank_i]], w=[b_rstd], scale=1.0 / D, bias=EPS)
        ACT(rstd[:, 0:n], rstd[:, 0:n], AF.Exp, r=[b_rstd], w=[b_rstd], scale=-0.5)
        for kc in range(KC):
            t = tmp[kc % len(tmp)]
            bt = b_tmp[kc % len(tmp)]
            TT("dve", t[:, 0:n], x_c[:, kc, 0:n], rstd[:, 0:n], ALU.mult, r=[b_x, b_rstd], w=[bt])
            if final:
                ACT(out_fn(kc), t[:, 0:n], AF.Identity, r=[bt, b_small], w=[b_out], scale=a_fn(kc))
            else:
                if kc % 2 == 1:
                    TS(out_fn(kc), t[:, 0:n], a_fn(kc), b_fn(kc), ALU.mult, ALU.add, r=[bt] + list(rb), w=[b_out2], eng="pool")
                else:
                    ACT(out_fn(kc), t[:, 0:n], AF.Identity, r=[bt] + list(rb), w=[b_out], scale=a_fn(kc), bias=b_fn(kc))

    def phase_in(l, xsrc):
        with ExitStack() as ps:
            hT = sb("hT_sb", [128, KC, NT], BF16, ps)
            b_hT = Buf()
            rstd = sb("rstd", [128, 512], F32, ps)
            b_rstd = Buf()
            tmp = [sb("tmp%d" % i, [128, 512], F32, ps) for i in range(4)]
            b_tmp = [Buf() for _ in range(4)]
            with ExitStack() as p1:
                xc = [sb("xc%d" % i, [128, KC, 256], F32, p1) for i in range(2)]
                b_xc = [Buf(), Buf()]
                sq = sb("sq", [128, KC, 256], BF16, p1)
                b_sq = Buf()
                for ci, (t0, n, cls) in enumerate(tok_chunks(NT, 256)):
                    x_c = xc[ci % 2]
                    bx = b_xc[ci % 2]
                    S.dma("sp", x_c[:, :, 0:n], xsrc.rearrange("k p t -> p k t")[:, :, t0:t0 + n], w=[bx])
                    norm_mod(x_c, bx, n, sq, b_sq, rstd, b_rstd, tmp, b_tmp, 6,
                             lambda kc: A1(l, kc, cls), lambda kc: MOD(l, 0, kc, cls),
                             lambda kc: hT[:, kc, t0:t0 + n], b_hT, rb=[b_derl[l], b_modl[l]])
                S.barrier()
            ropet = sb("ropet", [128, 4, NT], F32, ps)
            b_rope = Buf()
            S.dma("sp", ropet[:, :, :], rope_d[:, :, :], w=[b_rope])
            at = [sb("at%d" % i, [128, 512], F32, ps) for i in range(2)]
            b_at = [Buf(), Buf()]
            bt_ = [sb("bt%d" % i, [128, 512], F32, ps) for i in range(2)]
            b_bt = [Buf(), Buf()]
            qn = sb("qn", [128, 512], F32, ps)
            b_qn = Buf()
            sqb = sb("sqb", [128, 512], BF16, ps)
            b_sqb = Buf()
            stage = [sb("stage%d" % i, [128, NT], BF16, ps) for i in range(2)]
            b_stage = [Buf(), Buf()]
            vst = [sb("vst%d" % i, [128, 512], BF16, ps) for i in range(3)]
            b_vst = [Buf(), Buf(), Buf()]
            loads = [win_d[l].rearrange("(k p) n -> p k n", p=128)[:, :, t * 512:(t + 1) * 512] for t in range(10)]
            ws = WStream(loads, 16)
            sc_h = 128.0 ** -0.5
            sc_d = 64.0 ** -0.5
            plan = [
                (0, [(j, "q", j, "nr", None) for j in range(4)], None),
                (1, [(j, "k", j, "nr", None) for j in range(2)], (256, 256, 0)),
                (2, [(j, "q", 4 + j, "plain", sc_h) for j in range(4)], None),
                (3, [(j, "k", 2 + j, "plain", 1.0) for j in range(4)], None),
                (4, [], (0, 512, 256)),
                (5, [(j, "q", 8 + j, "rd", 1.0) for j in range(4)], None),
                (6, [(j, "k", 6 + j, "rd", 1.0) for j in range(4)], None),
                (7, [], (0, 512, 768)),
                (8, [(j, "q", 12 + j, "rh", sc_h) for j in range(4)], None),
                (9, [(j, "k", 10 + j, "rh", 1.0) for j in range(2)], (256, 256, 1280)),
            ]
            gi = 0
            bi = 0
            vi = 0
            tchunks = tok_chunks(NT, 512)
            for (t, fblocks, vspec) in plan:
                wt, bw = ws.get(t)
                for (j, kind, dst, mode, scale) in fblocks:
                    stg = stage[bi % 2]
                    bstg = b_stage[bi % 2]
                    bi += 1
                    for (t0, n, cls) in tchunks:
                        bk = gi % 3
                        gi += 1
                        pb = banks[bk]

                        def grp():
                            ins = None
                            for kc in range(KC):
                                ins = nc.tensor.matmul(pb[:, 0:n], lhsT=wt[:, kc * 512 + j * 128: kc * 512 + (j + 1) * 128],
                                                       rhs=hT[:, kc, t0:t0 + n], start=(kc == 0), stop=(kc == KC - 1))
                            return ins
                        MMG(grp, r=[bw, b_hT], w=[b_bank[bk]])
                        bpb = b_bank[bk]
                        if mode == "plain":
                            ACT(stg[:, t0:t0 + n], pb[:, 0:n], AF.Identity, r=[bpb], w=[bstg], scale=scale)
                            continue
                        a_t = at[gi % 2]
                        ba = b_at[gi % 2]
                        b_t = bt_[gi % 2]
                        bb = b_bt[gi % 2]
                        if mode == "nr":
                            ACT(sqb[:, 0:n], pb[:, 0:n], AF.Square, r=[bpb], w=[b_sqb])
                            MMG(lambda: nc.tensor.matmul(banks[6][:, 0:n], lhsT=ones, rhs=sqb[:, 0:n], start=True, stop=True),
                                r=[b_sqb, b_cst], w=[b_bank[6]])
                            ACT(rstd[:, 0:n], banks[6][:, 0:n], AF.Ln, r=[b_bank[6]], w=[b_rstd], scale=1.0 / 128, bias=EPS)
                            ACT(rstd[:, 0:n], rstd[:, 0:n], AF.Exp, r=[b_rstd], w=[b_rstd], scale=-0.5)
                            g = der[:, l, 64:65] if kind == "q" else smv("gk%d" % l)
                            STT(qn[:, 0:n], pb[:, 0:n], g, rstd[:, 0:n], ALU.mult, ALU.mult, r=[bpb, b_rstd, b_derl[l], b_small], w=[b_qn])
                            src = qn
                            bsrc = b_qn
                            tC, tS, step = 0, 1, 64
                            TT("dve", a_t[:, 0:n], src[:, 0:n], ropet[:, tC, t0:t0 + n], ALU.mult, r=[bsrc, b_rope], w=[ba])
                            for p0 in range(0, 128, step):
                                po = p0 + step if (p0 // step) % 2 == 0 else p0 - step
                                TT("dve", b_t[po:po + step, 0:n], src[p0:p0 + step, 0:n], ropet[p0:p0 + step, tS, t0:t0 + n], ALU.mult,
                                   r=[bsrc, b_rope], w=[bb])
                        else:
                            tC, tS, step = (0, 1, 64) if mode == "rh" else (2, 3, 32)
                            STT(a_t[:, 0:n], pb[:, 0:n], float(scale), ropet[:, tC, t0:t0 + n], ALU.mult, ALU.mult, r=[bpb, b_rope], w=[ba])
                            for p0 in range(0, 128, step):
                                po = p0 + step if (p0 // step) % 2 == 0 else p0 - step
                                STT(b_t[po:po + step, 0:n], pb[p0:p0 + step, 0:n], float(scale), ropet[p0:p0 + step, tS, t0:t0 + n],
                                    ALU.mult, ALU.mult, r=[bpb, b_rope], w=[bb])
                        TT("pool", stg[:, t0:t0 + n], a_t[:, 0:n], b_t[:, 0:n], ALU.add, r=[ba, bb], w=[bstg])
                    dstT = qT_d if kind == "q" else kT_d
                    S.dma("sp", dstT[dst], stg[:, :], r=[bstg])
                if vspec is not None:
                    c0, ncol, vcol = vspec
                    for tt in range(18):
                        bk = gi % 3
                        gi += 1
                        pb = banks[bk]

                        def grpv():
                            ins = None
                            for kc in range(KC):
                                ins = nc.tensor.matmul(pb[:, 0:ncol], lhsT=hT[:, kc, tt * 128:(tt + 1) * 128],
                                                       rhs=wt[:, kc * 512 + c0: kc * 512 + c0 + ncol], start=(kc == 0), stop=(kc == KC - 1))
                            return ins
                        MMG(grpv, r=[bw, b_hT], w=[b_bank[bk]])
                        vs = vst[vi % 3]
                        bvs = b_vst[vi % 3]
                        vi += 1
                        if tt % 2 == 0:
                            ACT(vs[:, 0:ncol], pb[:, 0:ncol], AF.Identity, r=[b_bank[bk]], w=[bvs])
                        else:
                            S.op("dve", lambda: nc.vector.tensor_copy(out=vs[:, 0:ncol], in_=pb[:, 0:ncol]), r=[b_bank[bk]], w=[bvs])
                        S.dma("sp", v_d[tt][:, vcol:vcol + ncol], vs[:, 0:ncol], r=[bvs])
            S.barrier()

    def phase_attn(l, with_ctx, hook=None):
        with ExitStack() as ps:
            qs = sb("qs", [128, 4, NT], BF16, ps)
            ks = sb("ks", [128, 4, NT], BF16, ps)
            vs = sb("vs", [128, 18, 512], BF16, ps)
            b_q, b_k, b_v = Buf(), Buf(), Buf()
            ost = sb("ost", [128, 4, NT], BF16, ps)
            b_ost = Buf()
            dm = sb("dm", [128, 6, 512], BF16, ps)
            b_dm = Buf()
            S.dma("pool", dm[:, :, :], dmask_d[:, :, :], w=[b_dm])
            NPT = 4
            pt2 = [sb("pt%d" % i, [128, 2, 512], BF16, ps) for i in range(NPT)]
            b_pt = [Buf() for _ in range(NPT)]
            b_pair = [Buf(), Buf()]
            pd = [sb("pd%d" % i, [128, 512], BF16, ps) for i in range(NPT)]
            b_pd = [Buf() for _ in range(NPT)]
            NNB = 8
            nbt = [sb("nbt%d" % i, [128, 512], BF16, ps) for i in range(NNB)]
            b_nbt = [Buf() for _ in range(NNB)]
            rec = [sb("rec%d" % i, [128, 512], F32, ps) for i in range(2)]
            b_rec = [Buf(), Buf()]
            o1 = sb("o1", [128, 512], F32, ps)
            b_o1 = Buf()
            o2 = sb("o2", [128, 512], F32, ps)
            b_o2 = Buf()
            sqb = sb("asqb", [128, 512], BF16, ps)
            b_sqb = Buf()
            rstd = sb("arstd", [128, 512], F32, ps)
            b_rstd = Buf()
            ctr = {"s": 0, "o": 0, "p": 0, "nb": 0, "r": 0}
            O_BANKS = [4, 5]
            D_BANKS = [6, 7]
            N_BANK = 7

            def attend(qap, ktl, scale):
                nq = qap.shape[-1]
                ob = O_BANKS[ctr["o"] % 2]
                db = D_BANKS[ctr["o"] % 2]
                ctr["o"] += 1
                nk = len(ktl)
                npair = (nk + 1) // 2

                AHEAD = 4
                bslot = {}

                def issue_bias(j):
                    if j < nk and ktl[j][2] is not None and ktl[j][2][0] == "dram":
                        i = ctr["nb"] % NNB
                        ctr["nb"] += 1
                        S.dma("pool", nbt[i][:, 0:nq], ktl[j][2][1], w=[b_nbt[i]])
                        bslot[j] = i
                for jj_ in range(AHEAD):
                    issue_bias(jj_)

                def s_pair(i):
                    pk = ctr["s"] % 2
                    ctr["s"] += 1
                    js = [j for j in (2 * i, 2 * i + 1) if j < nk]
                    rr = [b_k, b_q]
                    items = []
                    for c, j in enumerate(js):
                        kap, vap, bias = ktl[j]
                        issue_bias(j + AHEAD)
                        bsb = None
                        if bias is not None:
                            if bias[0] == "dram":
                                bi_ = bslot[j]
                                bsb = nbt[bi_][:, 0:nq]
                                rr = rr + [b_nbt[bi_], b_cst]
                            else:
                                bsb = bias[1]
                                rr = rr + [b_dm, b_cst]
                        items.append((c, kap, bsb))

                    def g():
                        ins = None
                        for (c, kap, bsb) in items:
                            ins = nc.tensor.matmul(bank2[pk][:, c, 0:nq], lhsT=kap, rhs=qap, start=True, stop=(bsb is None))
                            if bsb is not None:
                                ins = nc.tensor.matmul(bank2[pk][:, c, 0:nq], lhsT=ident, rhs=bsb, start=False, stop=True)
                        return ins
                    MMG(g, r=rr, w=[b_pair[pk]])
                    return pk, js

                def emit_den(i_, drhs_, rbuf_):
                    MMG(lambda: nc.tensor.matmul(banks[db][:, 0:nq], lhsT=ones, rhs=drhs_, start=(i_ == 0), stop=(i_ == npair - 1)),
                        r=[rbuf_, b_cst], w=[b_bank[db]])
                pend = None
                nxt = s_pair(0)
                for i in range(npair):
                    cur, js = nxt
                    if i + 1 < npair:
                        nxt = s_pair(i + 1)
                    pi = ctr["p"] % NPT
                    ctr["p"] += 1
                    nt_ = len(js)
                    ACT(pt2[pi][:, 0:nt_, 0:nq], bank2[cur][:, 0:nt_, 0:nq], AF.Exp, r=[b_pair[cur]], w=[b_pt[pi]], scale=scale)

                    def g2():
                        ins = None
                        for c, j in enumerate(js):
                            ins = nc.tensor.matmul(banks[ob][:, 0:nq], lhsT=ktl[j][1], rhs=pt2[pi][:, c, 0:nq], start=(j == 0), stop=(j == nk - 1))
                        return ins
                    MMG(g2, r=[b_v, b_pt[pi]], w=[b_bank[ob]])
                    if nt_ == 2:
                        TT("dve", pd[pi][:, 0:nq], pt2[pi][:, 0, 0:nq], pt2[pi][:, 1, 0:nq], ALU.add, r=[b_pt[pi]], w=[b_pd[pi]])
                        drhs, rbuf = pd[pi][:, 0:nq], b_pd[pi]
                    else:
                        drhs, rbuf = pt2[pi][:, 0, 0:nq], b_pt[pi]
                    if pend is not None:
                        emit_den(*pend)
                    pend = (i, drhs, rbuf)
                emit_den(*pend)
                return ob, db

            def recip(db, nq, sink_ap=None):
                ri = ctr["r"] % 2
                ctr["r"] += 1
                if sink_ap is not None:
                    TS(rec[ri][:, 0:nq], banks[db][:, 0:nq], sink_ap, None, ALU.add, None, r=[b_bank[db], b_derl[l]], w=[b_rec[ri]])
                    S.op("dve", lambda: nc.vector.reciprocal(out=rec[ri][:, 0:nq], in_=rec[ri][:, 0:nq]), r=[b_rec[ri]], w=[b_rec[ri]])
                else:
                    S.op("dve", lambda: nc.vector.reciprocal(out=rec[ri][:, 0:nq], in_=banks[db][:, 0:nq]), r=[b_bank[db]], w=[b_rec[ri]])
                return ri

            qchunks = [(t0, n, 0) for (t0, n) in chunks(0, SX, 512)]
            if with_ctx:
                qchunks.append((SX, SC, 1))
            for m in range(4):
                qb0 = 4 * m
                kb0, nkb = [(0, 2), (2, 4), (6, 4), (10, 2)][m]
                vc0, vw = [(0, 256), (256, 512), (768, 512), (1280, 256)][m]
                S.dma("sp", qs[:, :, :], qT_d[qb0:qb0 + 4].rearrange("k p t -> p k t"), w=[b_q])
                S.dma("sp", ks[:, 0:nkb, :], kT_d[kb0:kb0 + nkb].rearrange("k p t -> p k t"), w=[b_k])
                S.dma("sp", vs[:, :, 0:vw], v_d[:, :, vc0:vc0 + vw].rearrange("t p c -> p t c"), w=[b_v])
                for (q0, nq, qcls) in qchunks:
                    qi = q0 // 512
                    if m != 2:
                        for h in range(4):
                            if hook is not None and m == 0:
                                next(hook, None)
                            if m in (0, 3):
                                kvh = h // 2
                            else:
                                kvh = h
                            if qcls == 1:
                                tiles = [(16, None), (17, None)]
                            elif m == 0:
                                tiles = [(j, None) for j in range(18)]
                            elif m == 1:
                                tiles = [(j, ("dram", nbias_d[l, h, NB_OFF[qi] + jj])) for jj, j in enumerate(NB_TILES[qi])]
                                tiles += [(16, None), (17, None)]
                            else:
                                tiles = []
                                for jj in range(6):
                                    j = 4 * qi - 1 + jj
                                    if 0 <= j < 16:
                                        tiles.append((j, ("sb", dm[:, jj, :])))
                                tiles += [(16, None), (17, None)]
                            ktl = [(ks[:, kvh, j * 128:(j + 1) * 128], vs[:, j, kvh * 128:(kvh + 1) * 128], b) for (j, b) in tiles]
                            scale = 1.0
                            ob, db = attend(qs[:, h, q0:q0 + nq], ktl, scale)
                            ri = recip(db, nq, der[:, l, 66 + h:67 + h] if m == 3 else None)
                            TT("dve", ost[:, h, q0:q0 + nq], banks[ob][:, 0:nq], rec[ri][:, 0:nq], ALU.mult,
                               r=[b_bank[ob], b_rec[ri]], w=[b_ost])
                    else:
                        for h in range(4):
                            if hook is not None:
                                next(hook, None)
                                next(hook, None)
                            tiles = [16, 17] if qcls == 1 else list(range(18))
                            for sub in range(2):
                                p0 = 64 * sub
                                ktl = [(ks[p0:p0 + 64, h, j * 128:(j + 1) * 128], vs[:, j, h * 128:(h + 1) * 128], None) for j in tiles]
                                ob, db = attend(qs[p0:p0 + 64, h, q0:q0 + nq], ktl, 64.0 ** -0.5)
                                ri = recip(db, nq)
                                dst, bdst = (o1, b_o1) if sub == 0 else (o2, b_o2)
                                TT("dve", dst[:, 0:nq], banks[ob][:, 0:nq], rec[ri][:, 0:nq], ALU.mult,
                                   r=[b_bank[ob], b_rec[ri]], w=[bdst])
                            STT(o1[:, 0:nq], o2[:, 0:nq], der[:, l, 70:71], o1[:, 0:nq], ALU.mult, ALU.add, r=[b_o1, b_o2, b_derl[l]], w=[b_o1])
                            ACT(sqb[:, 0:nq], o1[:, 0:nq], AF.Square, r=[b_o1], w=[b_sqb])
                            MMG(lambda: nc.tensor.matmul(banks[N_BANK][:, 0:nq], lhsT=ones, rhs=sqb[:, 0:nq], start=True, stop=True),
                                r=[b_sqb, b_cst], w=[b_bank[N_BANK]])
                            ACT(rstd[:, 0:nq], banks[N_BANK][:, 0:nq], AF.Ln, r=[b_bank[N_BANK]], w=[b_rstd], scale=1.0 / 128, bias=EPS)
                            ACT(rstd[:, 0:nq], rstd[:, 0:nq], AF.Exp, r=[b_rstd], w=[b_rstd], scale=-0.5)
                            STT(ost[:, h, q0:q0 + nq], o1[:, 0:nq], der[:, l, 71:72], rstd[:, 0:nq], ALU.mult, ALU.mult,
                                r=[b_o1, b_rstd, b_derl[l]], w=[b_ost])
                if hook is not None and m == 3:
                    for _ in hook:
                        pass
                tq = NT if with_ctx else SX
                S.dma("sp", oT_d[qb0:qb0 + 4].rearrange("k p t -> p k t")[:, :, 0:tq], ost[:, :, 0:tq], r=[b_ost])
                S.barrier()

    def phase_out(l, xsrc, with_ctx):
        with ExitStack() as ps:
            wo = sb("wo", [128, KC, D], BF16, ps)
            b_wo = Buf()
            for i in range(4):
                S.dma("pool", wo[:, :, i * 512:(i + 1) * 512], wout_d[l].rearrange("(k p) n -> p k n", p=128)[:, :, i * 512:(i + 1) * 512], w=[b_wo])
            oc = [sb("oc%d" % i, [128, KC, 256], BF16, ps) for i in range(2)]
            b_oc = [Buf(), Buf()]
            xc = [sb("xc%d" % i, [128, KC, 256], F32, ps) for i in range(2)]
            b_xc = [Buf(), Buf()]
            sq = sb("sq", [128, KC, 256], BF16, ps)
            b_sq = Buf()
            rstd = sb("rstd", [128, 256], F32, ps)
            b_rstd = Buf()
            hst = [sb("hst%d" % i, [128, KC, 256], BF16, ps) for i in range(2)]
            b_hst = [Buf(), Buf()]
            b_hst2 = [Buf(), Buf()]
            tmpb = sb("tmpb", [128, KC, 256], F32, ps)
            b_tmpb = [Buf() for _ in range(KC)]
            gi = 0
            chs = tok_chunks(NT if with_ctx else SX, 256)
            b_xcn = [[Buf() for _ in range(KC)] for _ in range(2)]
            b_sqn = [Buf() for _ in range(KC)]

            def loads(ci):
                t0, n, cls = chs[ci]
                S.dma("sp", oc[ci % 2][:, :, 0:n], oT_d.rearrange("k p t -> p k t")[:, :, t0:t0 + n], w=[b_oc[ci % 2]])
                S.dma("sp", xc[ci % 2][:, :, 0:n], xsrc.rearrange("k p t -> p k t")[:, :, t0:t0 + n], w=b_xcn[ci % 2])
            loads(0)
            for ci, (t0, n, cls) in enumerate(chs):
                if ci + 1 < len(chs):
                    loads(ci + 1)
                o_c, bo = oc[ci % 2], b_oc[ci % 2]
                x_c, bxs = xc[ci % 2], b_xcn[ci % 2]
                h_s, bh = hst[ci % 2], b_hst[ci % 2]

                def ones_mm(k_):
                    MMG(lambda: nc.tensor.matmul(banks[6][:, 0:n], lhsT=ones, rhs=sq[:, k_, 0:n], start=(k_ == 0), stop=(k_ == KC - 1)),
                        r=[b_sqn[k_], b_cst], w=[b_bank[6]])
                for nb in range(KC):
                    bk = gi % 4
                    gi += 1
                    pb = banks[bk]

                    def grp():
                        ins = None
                        for kc in range(KC):
                            ins = nc.tensor.matmul(pb[:, 0:n], lhsT=wo[:, kc, nb * 128:(nb + 1) * 128], rhs=o_c[:, kc, 0:n],
                                                   start=(kc == 0), stop=(kc == KC - 1))
                        return ins
                    MMG(grp, r=[b_wo, bo], w=[b_bank[bk]])
                    STT(x_c[:, nb, 0:n], pb[:, 0:n], MOD(l, 2, nb, cls), x_c[:, nb, 0:n], ALU.mult, ALU.add,
                        r=[b_bank[bk], b_modl[l], bxs[nb]], w=[bxs[nb]])
                    ACT(sq[:, nb, 0:n], x_c[:, nb, 0:n], AF.Square, r=[bxs[nb]], w=[b_sqn[nb]])
                for nb in range(KC):
                    ones_mm(nb)
                S.dma("sp", xres.rearrange("k p t -> p k t")[:, :, t0:t0 + n], x_c[:, :, 0:n], r=bxs)
                ACT(rstd[:, 0:n], banks[6][:, 0:n], AF.Ln, r=[b_bank[6]], w=[b_rstd], scale=1.0 / D, bias=EPS)
                ACT(rstd[:, 0:n], rstd[:, 0:n], AF.Exp, r=[b_rstd], w=[b_rstd], scale=-0.5)
                for kc in range(KC):
                    t = tmpb[:, kc, :]
                    bt = b_tmpb[kc]
                    TT("dve", t[:, 0:n], x_c[:, kc, 0:n], rstd[:, 0:n], ALU.mult, r=[bxs[kc], b_rstd], w=[bt])
                    if kc % 2 == 1:
                        TS(h_s[:, kc, 0:n], t[:, 0:n], A2(l, kc, cls), MOD(l, 3, kc, cls), ALU.mult, ALU.add,
                           r=[bt, b_derl[l], b_modl[l]], w=[b_hst2[ci % 2]], eng="pool")
                    else:
                        ACT(h_s[:, kc, 0:n], t[:, 0:n], AF.Identity, r=[bt, b_derl[l], b_modl[l]], w=[bh],
                            scale=A2(l, kc, cls), bias=MOD(l, 3, kc, cls))
                S.dma("sp", hT_d.rearrange("k p t -> p k t")[:, :, t0:t0 + n], h_s[:, :, 0:n], r=[bh, b_hst2[ci % 2]])
            S.barrier()

    def phase_mlp(l, with_ctx, last):
        with ExitStack() as ps:
            TSUP = 1152 if with_ctx else 1024
            acc = sb("acc", [128, KC, TSUP], F32, ps)
            b_acc = Buf()
            h2 = sb("h2", [128, KC, TSUP], BF16, ps)
            b_h2 = Buf()
            ug = sb("ug", [128, 8, TSUP], BF16, ps)
            b_ug = Buf()
            rl = [sb("rl%d" % i, [128, 512], F32, ps) for i in range(2)]
            b_rl = [Buf(), Buf()]
            if last:
                sq = sb("fsq", [128, KC, 128], BF16, ps)
                b_sq = Buf()
                rstd = sb("frstd", [128, 512], F32, ps)
                b_rstd = Buf()
                tmp = [sb("ftmp%d" % i, [128, 512], F32, ps) for i in range(4)]
                b_tmp = [Buf() for _ in range(4)]
                fo = [sb("fo%d" % i, [128, KC, 128], F32, ps) for i in range(2)]
                b_fo = [Buf(), Buf()]
            gi = 0
            ri = 0
            for s in range(2):
                ts0 = s * TSUP
                ts1 = ts0 + TSUP
                subs = [(t0, n, cls) for (t0, n, cls) in tok_chunks(NT, 512) if False]
                subs = []
                for (a, b_, cls) in ((ts0, min(ts1, SX), 0), (max(ts0, SX), ts1, 1)):
                    if b_ > a:
                        subs += [(t, n, cls) for (t, n) in chunks(a, b_, 512)]
                S.dma("sp", acc[:, :, 0:TSUP], xres.rearrange("k p t -> p k t")[:, :, ts0:ts1], w=[b_acc])
                S.dma("sp", h2[:, :, 0:TSUP], hT_d.rearrange("k p t -> p k t")[:, :, ts0:ts1], w=[b_h2])
                loads = []
                for g in range(8):
                    for i in range(2):
                        loads.append(("up", wup_d[l].rearrange("(k p) n -> p k n", p=128)[:, :, g * 1024 + i * 512: g * 1024 + (i + 1) * 512]))
                    for i in range(2):
                        loads.append(("dn", wdown_d[l][g * 1024:(g + 1) * 1024, i * 1024:(i + 1) * 1024].rearrange("(k p) n -> p k n", p=128)))

                class MixStream(WStream):
                    def _issue(self2, i):
                        s_ = ring_ctr[0] % 3
                        ring_ctr[0] += 1
                        kind, ap = self2.loads[i]
                        a = 16 if kind == "up" else 8
                        dst = ring[s_][:, :].rearrange("p (a b) -> p a b", a=a)
                        S.dma("pool", dst, ap, w=[b_ring[s_]])
                        self2.slots[i] = s_
                ws = MixStream(loads, 16)
                for g in range(8):
                    for i in range(2):
                        wt, bw = ws.get(g * 4 + i)
                        for j in range(4):
                            hc = i * 4 + j
                            for (t0, n, cls) in subs:
                                bk = gi % 4
                                gi += 1
                                pb = banks[bk]
                                lo = t0 - ts0

                                def grp():
                                    ins = None
                                    for kc in range(KC):
                                        ins = nc.tensor.matmul(pb[:, 0:n], lhsT=wt[:, kc * 512 + j * 128: kc * 512 + (j + 1) * 128],
                                                               rhs=h2[:, kc, lo:lo + n], start=(kc == 0), stop=(kc == KC - 1))
                                    return ins
                                MMG(grp, r=[bw, b_h2], w=[b_bank[bk]])
                                r_ = rl[ri % 2]
                                br = b_rl[ri % 2]
                                ri += 1
                                ACT(r_[:, 0:n], pb[:, 0:n], AF.Relu, r=[b_bank[bk]], w=[br])
                                TT("pool", ug[:, hc, lo:lo + n], r_[:, 0:n], r_[:, 0:n], ALU.mult, r=[br], w=[b_ug])
                    for i in range(2):
                        wt, bw = ws.get(g * 4 + 2 + i)
                        for nbl in range(8):
                            nb = i * 8 + nbl
                            for (t0, n, cls) in subs:
                                bk = 4 + gi % 4
                                gi += 1
                                pb = banks[bk]
                                lo = t0 - ts0

                                def grpd():
                                    ins = None
                                    for hc in range(8):
                                        ins = nc.tensor.matmul(pb[:, 0:n], lhsT=wt[:, hc * 1024 + nbl * 128: hc * 1024 + (nbl + 1) * 128],
                                                               rhs=ug[:, hc, lo:lo + n], start=(hc == 0), stop=(hc == 7))
                                    return ins
                                MMG(grpd, r=[bw, b_ug], w=[b_bank[bk]])
                                STT(acc[:, nb, lo:lo + n], pb[:, 0:n], MOD(l, 5, nb, cls), acc[:, nb, lo:lo + n], ALU.mult, ALU.add,
                                    r=[b_bank[bk], b_modl[l], b_acc], w=[b_acc])
                if not last:
                    S.dma("sp", xres.rearrange("k p t -> p k t")[:, :, ts0:ts1], acc[:, :, 0:TSUP], r=[b_acc])
                else:
                    go, _ = _SM["gfin"]
                    for ci, (t0, n) in enumerate(chunks(0, TSUP, 128)):
                        f_o, bf = fo[ci % 2], b_fo[ci % 2]
                        norm_mod(acc[:, :, t0:t0 + n], b_acc, n, sq, b_sq, rstd, b_rstd, tmp, b_tmp, 3,
                                 lambda kc: small[:, go + kc: go + kc + 1], None,
                                 lambda kc: f_o[:, kc, 0:n], bf, final=True)
                        S.dma("sp", out_d.rearrange("k p t -> p k t")[:, :, ts0 + t0: ts0 + t0 + n], f_o[:, :, 0:n], r=[bf])
            S.barrier()

    for _ in mod_steps(0, 0, 8):
        pass
    der_a(0)
    S.barrier()
    dsteps = deferred_steps()
    stages = []
    for l in range(L):
        with_ctx = l < L - 1
        xsrc = xin if l == 0 else xres
        stages.append(("in%d" % l, lambda l=l, xsrc=xsrc: phase_in(l, xsrc)))
        stages.append(("attn%d" % l, lambda l=l, wc=with_ctx: phase_attn(l, wc, dsteps if l == 0 else None)))
        stages.append(("out%d" % l, lambda l=l, xsrc=xsrc, wc=with_ctx: phase_out(l, xsrc, wc)))
        stages.append(("mlp%d" % l, lambda l=l, wc=with_ctx: phase_mlp(l, wc, l == L - 1)))
    for name, fn in stages:
        fn()
        if stop_after == name:
            break
    S.barrier()
    es.close()
    return nc


def _pvec(v):
    return np.ascontiguousarray(np.asarray(v, np.float32).reshape(KC, 128).T)


def _rope_tables():
    def axial(n_tok, dim):
        t = np.arange(n_tok)
        row = (t // 64).astype(np.float32)
        col = (t % 64).astype(np.float32)
        nf = dim // 4
        inv = (np.float32(10000.0) ** (-np.arange(nf, dtype=np.float32) / np.float32(nf))).astype(np.float32)
        ang = np.concatenate([row[:, None] * inv, col[:, None] * inv], axis=-1).astype(np.float32)
        return np.cos(ang).astype(np.float32), np.sin(ang).astype(np.float32)
    ch, sh = axial(SX, 128)
    cd, sd = axial(SX, 64)
    tab = np.zeros((128, 4, NT), np.float32)
    tab[:, 0, SX:] = 1.0
    tab[:, 2, SX:] = 1.0
    p = np.arange(128)
    tab[:, 0, :SX] = ch[:, p % 64].T
    sgn = np.where(p < 64, 1.0, -1.0).astype(np.float32)
    tab[:, 1, :SX] = sh[:, p % 64].T * sgn[:, None]
    tab[:, 2, :SX] = cd[:, p % 32].T
    sgn2 = np.where((p // 32) % 2 == 0, 1.0, -1.0).astype(np.float32)
    tab[:, 3, :SX] = sd[:, p % 32].T * sgn2[:, None]
    return tab


def _perm_cols():
    perm = np.arange(PROJ)
    h128 = np.concatenate([np.arange(0, 128, 2), np.arange(1, 128, 2)])
    h64 = np.concatenate([np.arange(0, 64, 2), np.arange(1, 64, 2)])
    for base, nh in ((0, 4), (512, 2), (4096, 4), (4608, 2)):
        for h in range(nh):
            b = base + h * 128
            perm[b:b + 128] = b + h128
    for base in (2560, 3072):
        for h in range(8):
            b = base + h * 64
            perm[b:b + 64] = b + h64
    return perm, h128


def _nb_bias(rpb):
    out = np.empty((4, NB_NT, 128, 512), np.float32)
    for qi in range(4):
        tq = 512 * qi + np.arange(512)
        qr, qc = tq // 64, tq % 64
        rs = np.clip(qr - 4, 0, 32 - 8)
        cs = np.clip(qc - 8, 0, 64 - 16)
        for jj, j in enumerate(NB_TILES[qi]):
            tk = 128 * j + np.arange(128)
            kr, kcol = tk // 64, tk % 64
            valid = ((kr[:, None] >= rs[None, :]) & (kr[:, None] < rs[None, :] + 8) &
                     (kcol[:, None] >= cs[None, :]) & (kcol[:, None] < cs[None, :] + 16))
            dr = np.clip(kr[:, None] - qr[None, :] + 7, 0, 14)
            dc = np.clip(kcol[:, None] - qc[None, :] + 15, 0, 30)
            for h in range(4):
                out[h, NB_OFF[qi] + jj] = np.where(valid, rpb[h][dr, dc], np.float32(NEG))
    return out


def _dmask():
    m = np.empty((128, 6, 512), np.float32)
    kk = np.arange(128)[:, None]
    qq = np.arange(512)[None, :]
    for jj in range(6):
        diff = (-128 + 128 * jj + kk) - qq
        m[:, jj, :] = np.where(np.abs(diff) <= 128, 0.0, NEG)
    return m


def prep_inputs(inp):
    f = lambda a: np.asarray(a, np.float32)
    x, c, ctx, c_ctx = f(inp["x"]), f(inp["c"]), f(inp["ctx"]), f(inp["c_ctx"])
    perm, h128 = _perm_cols()
    shared = {
        "rope": _rope_tables(),
        "consts": np.concatenate([np.eye(128, dtype=np.float32), np.ones((128, 128), np.float32)], axis=1),
        "dmask": _dmask(),
        "wmod": np.ascontiguousarray(f(inp["w_mod"])),
        "win": np.ascontiguousarray(f(inp["w_in"])[:, :, perm]),
        "wout": np.ascontiguousarray(f(inp["w_out"])),
        "wup": np.ascontiguousarray(f(inp["w_up"])),
        "wdown": np.ascontiguousarray(f(inp["w_down"])),
        "nbias": np.stack([_nb_bias(f(inp["na_rpb"])[l]) for l in range(L)]),
    }
    small0 = np.zeros((128, NS), np.float32)

    def put(name, arr):
        o, w = _SM[name]
        small0[:, o:o + w] = arr.reshape(128, w)
    for l in range(L):
        put("gmix%d" % l, _pvec(inp["g_mix"][l]))
        put("gmlp%d" % l, _pvec(inp["g_mlp"][l]))
        put("bmod%d" % l, np.ascontiguousarray(f(inp["b_mod"])[l].reshape(96, 128).T))
        put("gq%d" % l, f(inp["gqa_gq"])[l][h128][:, None])
        put("gk%d" % l, f(inp["gqa_gk"])[l][h128][:, None])
        put("gsub%d" % l, f(inp["diff_gsub"])[l][:, None])
        for n, key in (("lq1", "diff_lq1"), ("lk1", "diff_lk1"), ("lq2", "diff_lq2"), ("lk2", "diff_lk2")):
            put("%s%d" % (n, l), np.broadcast_to(f(inp[key])[l][None, :], (128, 64)).copy())
        put("sink%d" % l, np.broadcast_to(f(inp["swa_sink"])[l][None, :], (128, 4)).copy())
    put("gfin", _pvec(inp["g_final"]))
    in_maps = []
    for b in range(8):
        xt = np.concatenate([x[b], ctx[b]], axis=0)
        xin = np.ascontiguousarray(xt.T.reshape(KC, 128, NT))
        sm = small0.copy()
        o, w = _SM["cond"]
        cond = np.stack([_pvec(c[b]), _pvec(c_ctx)], axis=-1)
        sm[:, o:o + w] = cond.reshape(128, 32)
        d = dict(shared)
        d["xin"] = xin
        d["small"] = sm
        in_maps.append(d)
    return in_maps


_NC_CACHE = {}


def kernel(**inputs):
    in_maps = prep_inputs(inputs)
    if "nc" not in _NC_CACHE:
        _NC_CACHE["nc"] = build()
    nc = _NC_CACHE["nc"]
    res = run_bass_kernel_spmd(nc, in_maps, core_ids=list(range(8)))
    outs = []
    for b in range(8):
        o = np.asarray(res.results[b]["out"], np.float32)
        outs.append(o.reshape(D, SX).T)
    return np.ascontiguousarray(np.stack(outs, axis=0))
```

```python
import math
from contextlib import ExitStack
import numpy as np
import concourse.bass as bass
import concourse.mybir as mybir
from concourse.bass_utils import run_bass_kernel_spmd

F32 = mybir.dt.float32
BF16 = mybir.dt.bfloat16
AF = mybir.ActivationFunctionType
ALU = mybir.AluOpType
AX = mybir.AxisListType

D = 2048
SX = 2048
SC = 256
NT = SX + SC
L = 2
KC = 16
HID = 8192
PROJ = 5120
NEG = -30000.0
EPS = 1e-6
NB_TILES = {0: list(range(0, 6)), 1: list(range(2, 10)), 2: list(range(6, 14)), 3: list(range(10, 16))}
NB_OFF = {0: 0, 1: 6, 2: 14, 3: 22}
NB_NT = 28

_SM = {}
_off = 0


def _reg(name, w):
    global _off
    _SM[name] = (_off, w)
    _off += w


_reg("cond", 32)
for _l in range(L):
    _reg("gmix%d" % _l, 16)
    _reg("gmlp%d" % _l, 16)
    _reg("bmod%d" % _l, 96)
    _reg("gq%d" % _l, 1)
    _reg("gk%d" % _l, 1)
    _reg("gsub%d" % _l, 1)
    for _n in ("lq1", "lk1", "lq2", "lk2"):
        _reg("%s%d" % (_n, _l), 64)
    _reg("sink%d" % _l, 4)
_reg("gfin", 16)
NS = _off


def lam_init(l):
    return 0.8 - 0.6 * math.exp(-0.3 * l)


class Buf:
    __slots__ = ("w", "rs")

    def __init__(self):
        self.w = None
        self.rs = {}


class Sched:
    R = 8

    def __init__(self, nc):
        self.nc = nc
        self.engs = {"pe": nc.tensor, "act": nc.scalar, "dve": nc.vector, "pool": nc.gpsimd, "sp": nc.sync}
        self.sem = {e: nc.alloc_semaphore("sem_" + e) for e in ("pe", "act", "dve", "pool")}
        self.cnt = {e: 0 for e in self.sem}
        self.seen = {e: {} for e in self.engs}
        self.dsem = {q: [nc.alloc_semaphore("dsem_%s_%d" % (q, i)) for i in range(self.R)] for q in ("sp", "pool")}
        self.dcnt = {q: [0] * self.R for q in self.dsem}
        self.drr = {q: 0 for q in self.dsem}

    def _wait(self, eng, tok):
        if tok is None:
            return
        sem, key, val = tok
        if key == "pe" and eng == "pe":
            return
        if self.seen[eng].get(key, 0) >= val:
            return
        self.engs[eng].wait_ge(sem, val)
        self.seen[eng][key] = val

    def _deps(self, eng, r, w):
        for b in r:
            self._wait(eng, b.w)
        for b in w:
            self._wait(eng, b.w)
            for key, (sem, val) in b.rs.items():
                self._wait(eng, (sem, key, val))

    def _commit(self, tok, r, w):
        sem, key, val = tok
        for b in r:
            b.rs[key] = (sem, val)
        for b in w:
            b.w = tok
            b.rs = {}

    def op(self, eng, fn, r=(), w=()):
        self._deps(eng, r, w)
        ins = fn()
        self.cnt[eng] += 1
        ins.then_inc(self.sem[eng], 1)
        self._commit((self.sem[eng], eng, self.cnt[eng]), r, w)

    def dma(self, q, out, in_, r=(), w=()):
        i = self.drr[q]
        self.drr[q] = (i + 1) % self.R
        sem = self.dsem[q][i]
        key = (q, i)
        if self.dcnt[q][i] > 0:
            self._wait(q, (sem, key, 16 * self.dcnt[q][i]))
        self._deps(q, r, w)
        self.engs[q].dma_start(out=out, in_=in_).then_inc(sem, 16)
        self.dcnt[q][i] += 1
        self._commit((sem, key, 16 * self.dcnt[q][i]), r, w)

    def barrier(self):
        for e in self.engs:
            for c in self.sem:
                if self.cnt[c] > 0:
                    self._wait(e, (self.sem[c], c, self.cnt[c]))
            for q in self.dsem:
                for i in range(self.R):
                    if self.dcnt[q][i] > 0:
                        self._wait(e, (self.dsem[q][i], (q, i), 16 * self.dcnt[q][i]))


def chunks(t0, t1, step):
    out = []
    t = t0
    while t < t1:
        n = min(step, t1 - t)
        out.append((t, n))
        t += n
    return out


def tok_chunks(t_end, step):
    res = [(t, n, 0) for (t, n) in chunks(0, min(t_end, SX), step)]
    if t_end > SX:
        res += [(t, n, 1) for (t, n) in chunks(SX, t_end, step)]
    return res


def build(debug=False, stop_after=None):
    nc = bass.Bass("TRN2", target_bir_lowering=False)
    S = Sched(nc)
    es = ExitStack()

    def din(name, shape, dt=F32):
        return nc.dram_tensor(name, shape, dt, kind="ExternalInput").ap()

    def dscr(name, shape, dt):
        if debug:
            return nc.dram_tensor(name, shape, dt, kind="ExternalOutput").ap()
        return nc.dram_tensor(name, shape, dt).ap()

    xin = din("xin", [KC, 128, NT])
    small_d = din("small", [128, NS])
    rope_d = din("rope", [128, 4, NT])
    consts_d = din("consts", [128, 256])
    dmask_d = din("dmask", [128, 6, 512])
    wmod_d = din("wmod", [L, D, 6 * D])
    win_d = din("win", [L, D, PROJ])
    wout_d = din("wout", [L, D, D])
    wup_d = din("wup", [L, D, HID])
    wdown_d = din("wdown", [L, HID, D])
    nbias_d = din("nbias", [L, 4, NB_NT, 128, 512])
    out_d = nc.dram_tensor("out", [KC, 128, SX], F32, kind="ExternalOutput").ap()

    xres = dscr("xres", [KC, 128, NT], F32)
    hT_d = dscr("hT", [KC, 128, NT], BF16)
    qT_d = dscr("qT", [16, 128, NT], BF16)
    kT_d = dscr("kT", [12, 128, NT], BF16)
    v_d = dscr("vS", [18, 128, 1536], BF16)
    oT_d = dscr("oT", [KC, 128, NT], BF16)

    uid = [0]

    def sb(name, shape, dt, stack=None):
        uid[0] += 1
        return (stack or es).enter_context(nc.sbuf_tensor("s%d_%s" % (uid[0], name), shape, dt))

    small = sb("small", [128, NS], F32)
    b_small = Buf()
    cst = sb("cst", [128, 256], BF16)
    b_cst = Buf()
    ident = cst[:, 0:128]
    ones = cst[:, 128:256]
    modsb = sb("modsb", [128, L, 96, 2], F32)
    b_mod = Buf()
    der = sb("der", [128, L, 80], F32)
    b_der = Buf()
    condT = sb("condT", [128, 32], BF16)
    b_cond = Buf()
    ring = [sb("ring%d" % i, [128, 8192], BF16) for i in range(3)]
    b_ring = [Buf() for _ in range(3)]
    ring_ctr = [0]
    bank2 = [nc.alloc_psum_tensor("bank2_%d" % i, [128, 2, 512], F32) for i in range(4)]
    banks = [bank2[i // 2][:, i % 2, :] for i in range(8)]
    b_bank = [Buf() for _ in range(8)]

    def smv(name):
        o, w = _SM[name]
        return small[:, o:o + w]

    def ACT(out, in_, func, r, w, **kw):
        S.op("act", lambda: nc.scalar.activation(out=out, in_=in_, func=func, **kw), r=r, w=w)

    def TT(eng, out, in0, in1, op, r, w):
        e = nc.vector if eng == "dve" else nc.gpsimd
        S.op(eng, lambda: e.tensor_tensor(out=out, in0=in0, in1=in1, op=op), r=r, w=w)

    def STT(out, in0, scalar, in1, op0, op1, r, w):
        S.op("dve", lambda: nc.vector.scalar_tensor_tensor(out=out, in0=in0, scalar=scalar, in1=in1, op0=op0, op1=op1), r=r, w=w)

    def TS(out, in0, s1, s2, op0, op1, r, w, eng="dve"):
        e = nc.vector if eng == "dve" else nc.gpsimd
        if op1 is None:
            S.op(eng, lambda: e.tensor_scalar(out=out, in0=in0, scalar1=s1, scalar2=None, op0=op0), r=r, w=w)
        else:
            S.op(eng, lambda: e.tensor_scalar(out=out, in0=in0, scalar1=s1, scalar2=s2, op0=op0, op1=op1), r=r, w=w)

    def MMG(fn, r, w):
        S.op("pe", fn, r=r, w=w)

    class WStream:
        def __init__(self, loads, a):
            self.loads = loads
            self.a = a
            self.issued = 0
            self.slots = {}

        def _issue(self, i):
            s = ring_ctr[0] % 3
            ring_ctr[0] += 1
            dst = ring[s][:, :].rearrange("p (a b) -> p a b", a=self.a)
            S.dma("pool", dst, self.loads[i], w=[b_ring[s]])
            self.slots[i] = s

        def get(self, i, ahead=2):
            while self.issued < min(len(self.loads), i + 1 + ahead):
                self._issue(self.issued)
                self.issued += 1
            s = self.slots[i]
            return ring[s], b_ring[s]

    S.dma("sp", small[:, :], small_d[:, :], w=[b_small])
    S.dma("pool", cst[:, :], consts_d[:, :], w=[b_cst])
    o_c, _ = _SM["cond"]
    ACT(condT[:, :], small[:, o_c:o_c + 32], AF.Silu, r=[b_small], w=[b_cond])

    b_modl = [Buf() for _ in range(L)]
    b_derl = [Buf() for _ in range(L)]
    lamt = sb("lamt", [128, 64], F32)
    b_lt = Buf()

    def mod_steps(l, t_lo, t_hi):
        loads = [wmod_d[l].rearrange("(k p) n -> p k n", p=128)[:, :, nt * 512:(nt + 1) * 512] for nt in range(t_lo, t_hi)]
        ws = WStream(loads, 16)
        pb = banks[7]
        for nt in range(t_lo, t_hi):
            wt, bw = ws.get(nt - t_lo)

            def grp():
                ins = None
                for j in range(4):
                    for kc in range(KC):
                        ins = nc.tensor.matmul(pb[:, j * 2:j * 2 + 2], lhsT=wt[:, kc * 512 + j * 128: kc * 512 + (j + 1) * 128],
                                               rhs=condT[:, kc * 2:kc * 2 + 2], start=(kc == 0), stop=(kc == KC - 1))
                return ins
            MMG(grp, r=[bw, b_cond], w=[b_bank[7]])
            pv = pb[:, 0:8].rearrange("p (n c) -> p n c", c=2)
            bm = smv("bmod%d" % l)
            for cls in range(2):
                TT("dve", modsb[:, l, nt * 4:(nt + 1) * 4, cls], pv[:, :, cls], bm[:, nt * 4:(nt + 1) * 4], ALU.add,
                   r=[b_bank[7], b_small], w=[b_modl[l]])
            yield

    def der_a(l):
        for cls in range(2):
            a1 = der[:, l, 0:32].rearrange("p (k c) -> p k c", c=2)[:, :, cls]
            STT(a1, modsb[:, l, 16:32, cls], 1.0, smv("gmix%d" % l), ALU.add, ALU.mult, r=[b_modl[l], b_small], w=[b_derl[l]])
        TS(der[:, l, 64:65], smv("gq%d" % l), 128.0 ** -0.5, None, ALU.mult, None, r=[b_small], w=[b_derl[l]])
        ACT(der[:, l, 66:70], smv("sink%d" % l), AF.Exp, r=[b_small], w=[b_derl[l]])
        TT("dve", lamt[:, :], smv("lq1%d" % l), smv("lk1%d" % l), ALU.mult, r=[b_small], w=[b_lt])
        S.op("dve", lambda: nc.vector.tensor_reduce(out=der[:, l, 72:73], in_=lamt[:, :], axis=AX.X, op=ALU.add), r=[b_lt], w=[b_derl[l]])
        TT("dve", lamt[:, :], smv("lq2%d" % l), smv("lk2%d" % l), ALU.mult, r=[b_small], w=[b_lt])
        S.op("dve", lambda: nc.vector.tensor_reduce(out=der[:, l, 73:74], in_=lamt[:, :], axis=AX.X, op=ALU.add), r=[b_lt], w=[b_derl[l]])
        ACT(der[:, l, 72:74], der[:, l, 72:74], AF.Exp, r=[b_derl[l]], w=[b_derl[l]])
        TT("dve", der[:, l, 74:75], der[:, l, 73:74], der[:, l, 72:73], ALU.subtract, r=[b_derl[l]], w=[b_derl[l]])
        TS(der[:, l, 70:71], der[:, l, 74:75], -lam_init(l), None, ALU.add, None, r=[b_derl[l]], w=[b_derl[l]])
        TS(der[:, l, 71:72], smv("gsub%d" % l), 1.0 - lam_init(l), None, ALU.mult, None, r=[b_small], w=[b_derl[l]])

    def der_b(l):
        for cls in range(2):
            a2 = der[:, l, 32:64].rearrange("p (k c) -> p k c", c=2)[:, :, cls]
            STT(a2, modsb[:, l, 64:80, cls], 1.0, smv("gmlp%d" % l), ALU.add, ALU.mult, r=[b_modl[l], b_small], w=[b_derl[l]])

    def deferred_steps():
        yield from mod_steps(0, 8, 24)
        der_b(0)
        yield
        yield from mod_steps(1, 0, 24)
        der_a(1)
        der_b(1)
        yield

    def A1(l, kc, cls):
        return der[:, l, kc * 2 + cls: kc * 2 + cls + 1]

    def A2(l, kc, cls):
        return der[:, l, 32 + kc * 2 + cls: 32 + kc * 2 + cls + 1]

    def MOD(l, j, kc, cls):
        return modsb[:, l, j * 16 + kc, cls:cls + 1]

    def norm_mod(x_c, b_x, n, sq, b_sq, rstd, b_rstd, tmp, b_tmp, bank_i, a_fn, b_fn, out_fn, b_out, final=False, rb=()):
        ACT(sq[:, :, 0:n], x_c[:, :, 0:n], AF.Square, r=[b_x], w=[b_sq])
        pb = banks[bank_i]
        b_out2 = Buf()

        def grp():
            ins = None
            for kc in range(KC):
                ins = nc.tensor.matmul(pb[:, 0:n], lhsT=ones, rhs=sq[:, kc, 0:n], start=(kc == 0), stop=(kc == KC - 1))
            return ins
        MMG(grp, r=[b_sq, b_cst], w=[b_bank[bank_i]])
        ACT(rstd[:, 0:n], pb[:, 0:n], AF.Ln, r=[b_bank[bank_i]], w=[b_rstd], scale=1.0 / D, bias=EPS)
        ACT(rstd[:, 0:n], rstd[:, 0:n], AF.Exp, r=[b_rstd], w=[b_rstd], scale=-0.5)
        for kc in range(KC):
            t = tmp[kc % len(tmp)]
            bt = b_tmp[kc % len(tmp)]
            TT("dve", t[:, 0:n], x_c[:, kc, 0:n], rstd[:, 0:n], ALU.mult, r=[b_x, b_rstd], w=[bt])
            if final:
                ACT(out_fn(kc), t[:, 0:n], AF.Identity, r=[bt, b_small], w=[b_out], scale=a_fn(kc))
            else:
                if kc % 2 == 1:
                    TS(out_fn(kc), t[:, 0:n], a_fn(kc), b_fn(kc), ALU.mult, ALU.add, r=[bt] + list(rb), w=[b_out2], eng="pool")
                else:
                    ACT(out_fn(kc), t[:, 0:n], AF.Identity, r=[bt] + list(rb), w=[b_out], scale=a_fn(kc), bias=b_fn(kc))

    def phase_in(l, xsrc):
        with ExitStack() as ps:
            hT = sb("hT_sb", [128, KC, NT], BF16, ps)
            b_hT = Buf()
            rstd = sb("rstd", [128, 512], F32, ps)
            b_rstd = Buf()
            tmp = [sb("tmp%d" % i, [128, 512], F32, ps) for i in range(4)]
            b_tmp = [Buf() for _ in range(4)]
            with ExitStack() as p1:
                xc = [sb("xc%d" % i, [128, KC, 256], F32, p1) for i in range(2)]
                b_xc = [Buf(), Buf()]
                sq = sb("sq", [128, KC, 256], BF16, p1)
                b_sq = Buf()
                for ci, (t0, n, cls) in enumerate(tok_chunks(NT, 256)):
                    x_c = xc[ci % 2]
                    bx = b_xc[ci % 2]
                    S.dma("sp", x_c[:, :, 0:n], xsrc.rearrange("k p t -> p k t")[:, :, t0:t0 + n], w=[bx])
                    norm_mod(x_c, bx, n, sq, b_sq, rstd, b_rstd, tmp, b_tmp, 6,
                             lambda kc: A1(l, kc, cls), lambda kc: MOD(l, 0, kc, cls),
                             lambda kc: hT[:, kc, t0:t0 + n], b_hT, rb=[b_derl[l], b_modl[l]])
                S.barrier()
            ropet = sb("ropet", [128, 4, NT], F32, ps)
            b_rope = Buf()
            S.dma("sp", ropet[:, :, :], rope_d[:, :, :], w=[b_rope])
            at = [sb("at%d" % i, [128, 512], F32, ps) for i in range(2)]
            b_at = [Buf(), Buf()]
            bt_ = [sb("bt%d" % i, [128, 512], F32, ps) for i in range(2)]
            b_bt = [Buf(), Buf()]
            qn = sb("qn", [128, 512], F32, ps)
            b_qn = Buf()
            sqb = sb("sqb", [128, 512], BF16, ps)
            b_sqb = Buf()
            stage = [sb("stage%d" % i, [128, NT], BF16, ps) for i in range(2)]
            b_stage = [Buf(), Buf()]
            vst = [sb("vst%d" % i, [128, 512], BF16, ps) for i in range(3)]
            b_vst = [Buf(), Buf(), Buf()]
            loads = [win_d[l].rearrange("(k p) n -> p k n", p=128)[:, :, t * 512:(t + 1) * 512] for t in range(10)]
            ws = WStream(loads, 16)
            sc_h = 128.0 ** -0.5
            sc_d = 64.0 ** -0.5
            plan = [
                (0, [(j, "q", j, "nr", None) for j in range(4)], None),
                (1, [(j, "k", j, "nr", None) for j in range(2)], (256, 256, 0)),
                (2, [(j, "q", 4 + j, "plain", sc_h) for j in range(4)], None),
                (3, [(j, "k", 2 + j, "plain", 1.0) for j in range(4)], None),
                (4, [], (0, 512, 256)),
                (5, [(j, "q", 8 + j, "rd", 1.0) for j in range(4)], None),
                (6, [(j, "k", 6 + j, "rd", 1.0) for j in range(4)], None),
                (7, [], (0, 512, 768)),
                (8, [(j, "q", 12 + j, "rh", sc_h) for j in range(4)], None),
                (9, [(j, "k", 10 + j, "rh", 1.0) for j in range(2)], (256, 256, 1280)),
            ]
            gi = 0
            bi = 0
            vi = 0
            tchunks = tok_chunks(NT, 512)
            for (t, fblocks, vspec) in plan:
                wt, bw = ws.get(t)
                for (j, kind, dst, mode, scale) in fblocks:
                    stg = stage[bi % 2]
                    bstg = b_stage[bi % 2]
                    bi += 1
                    for (t0, n, cls) in tchunks:
                        bk = gi % 3
                        gi += 1
                        pb = banks[bk]

                        def grp():
                            ins = None
                            for kc in range(KC):
                                ins = nc.tensor.matmul(pb[:, 0:n], lhsT=wt[:, kc * 512 + j * 128: kc * 512 + (j + 1) * 128],
                                                       rhs=hT[:, kc, t0:t0 + n], start=(kc == 0), stop=(kc == KC - 1))
                            return ins
                        MMG(grp, r=[bw, b_hT], w=[b_bank[bk]])
                        bpb = b_bank[bk]
                        if mode == "plain":
                            ACT(stg[:, t0:t0 + n], pb[:, 0:n], AF.Identity, r=[bpb], w=[bstg], scale=scale)
                            continue
                        a_t = at[gi % 2]
                        ba = b_at[gi % 2]
                        b_t = bt_[gi % 2]
                        bb = b_bt[gi % 2]
                        if mode == "nr":
                            ACT(sqb[:, 0:n], pb[:, 0:n], AF.Square, r=[bpb], w=[b_sqb])
                            MMG(lambda: nc.tensor.matmul(banks[6][:, 0:n], lhsT=ones, rhs=sqb[:, 0:n], start=True, stop=True),
                                r=[b_sqb, b_cst], w=[b_bank[6]])
                            ACT(rstd[:, 0:n], banks[6][:, 0:n], AF.Ln, r=[b_bank[6]], w=[b_rstd], scale=1.0 / 128, bias=EPS)
                            ACT(rstd[:, 0:n], rstd[:, 0:n], AF.Exp, r=[b_rstd], w=[b_rstd], scale=-0.5)
                            g = der[:, l, 64:65] if kind == "q" else smv("gk%d" % l)
                            STT(qn[:, 0:n], pb[:, 0:n], g, rstd[:, 0:n], ALU.mult, ALU.mult, r=[bpb, b_rstd, b_derl[l], b_small], w=[b_qn])
                            src = qn
                            bsrc = b_qn
                            tC, tS, step = 0, 1, 64
                            TT("dve", a_t[:, 0:n], src[:, 0:n], ropet[:, tC, t0:t0 + n], ALU.mult, r=[bsrc, b_rope], w=[ba])
                            for p0 in range(0, 128, step):
                                po = p0 + step if (p0 // step) % 2 == 0 else p0 - step
                                TT("dve", b_t[po:po + step, 0:n], src[p0:p0 + step, 0:n], ropet[p0:p0 + step, tS, t0:t0 + n], ALU.mult,
                                   r=[bsrc, b_rope], w=[bb])
                        else:
                            tC, tS, step = (0, 1, 64) if mode == "rh" else (2, 3, 32)
                            STT(a_t[:, 0:n], pb[:, 0:n], float(scale), ropet[:, tC, t0:t0 + n], ALU.mult, ALU.mult, r=[bpb, b_rope], w=[ba])
                            for p0 in range(0, 128, step):
                                po = p0 + step if (p0 // step) % 2 == 0 else p0 - step
                                STT(b_t[po:po + step, 0:n], pb[p0:p0 + step, 0:n], float(scale), ropet[p0:p0 + step, tS, t0:t0 + n],
                                    ALU.mult, ALU.mult, r=[bpb, b_rope], w=[bb])
                        TT("pool", stg[:, t0:t0 + n], a_t[:, 0:n], b_t[:, 0:n], ALU.add, r=[ba, bb], w=[bstg])
                    dstT = qT_d if kind == "q" else kT_d
                    S.dma("sp", dstT[dst], stg[:, :], r=[bstg])
                if vspec is not None:
                    c0, ncol, vcol = vspec
                    for tt in range(18):
                        bk = gi % 3
                        gi += 1
                        pb = banks[bk]

                        def grpv():
                            ins = None
                            for kc in range(KC):
                                ins = nc.tensor.matmul(pb[:, 0:ncol], lhsT=hT[:, kc, tt * 128:(tt + 1) * 128],
                                                       rhs=wt[:, kc * 512 + c0: kc * 512 + c0 + ncol], start=(kc == 0), stop=(kc == KC - 1))
                            return ins
                        MMG(grpv, r=[bw, b_hT], w=[b_bank[bk]])
                        vs = vst[vi % 3]
                        bvs = b_vst[vi % 3]
                        vi += 1
                        if tt % 2 == 0:
                            ACT(vs[:, 0:ncol], pb[:, 0:ncol], AF.Identity, r=[b_bank[bk]], w=[bvs])
                        else:
                            S.op("dve", lambda: nc.vector.tensor_copy(out=vs[:, 0:ncol], in_=pb[:, 0:ncol]), r=[b_bank[bk]], w=[bvs])
                        S.dma("sp", v_d[tt][:, vcol:vcol + ncol], vs[:, 0:ncol], r=[bvs])
            S.barrier()

    def phase_attn(l, with_ctx, hook=None):
        with ExitStack() as ps:
            qs = sb("qs", [128, 4, NT], BF16, ps)
            ks = sb("ks", [128, 4, NT], BF16, ps)
            vs = sb("vs", [128, 18, 512], BF16, ps)
            b_q, b_k, b_v = Buf(), Buf(), Buf()
            ost = sb("ost", [128, 4, NT], BF16, ps)
            b_ost = Buf()
            dm = sb("dm", [128, 6, 512], BF16, ps)
            b_dm = Buf()
            S.dma("pool", dm[:, :, :], dmask_d[:, :, :], w=[b_dm])
            NPT = 4
            pt2 = [sb("pt%d" % i, [128, 2, 512], BF16, ps) for i in range(NPT)]
            b_pt = [Buf() for _ in range(NPT)]
            b_pair = [Buf(), Buf()]
            pd = [sb("pd%d" % i, [128, 512], BF16, ps) for i in range(NPT)]
            b_pd = [Buf() for _ in range(NPT)]
            NNB = 8
            nbt = [sb("nbt%d" % i, [128, 512], BF16, ps) for i in range(NNB)]
            b_nbt = [Buf() for _ in range(NNB)]
            rec = [sb("rec%d" % i, [128, 512], F32, ps) for i in range(2)]
            b_rec = [Buf(), Buf()]
            o1 = sb("o1", [128, 512], F32, ps)
            b_o1 = Buf()
            o2 = sb("o2", [128, 512], F32, ps)
            b_o2 = Buf()
            sqb = sb("asqb", [128, 512], BF16, ps)
            b_sqb = Buf()
            rstd = sb("arstd", [128, 512], F32, ps)
            b_rstd = Buf()
            ctr = {"s": 0, "o": 0, "p": 0, "nb": 0, "r": 0}
            O_BANKS = [4, 5]
            D_BANKS = [6, 7]
            N_BANK = 7

            def attend(qap, ktl, scale):
                nq = qap.shape[-1]
                ob = O_BANKS[ctr["o"] % 2]
                db = D_BANKS[ctr["o"] % 2]
                ctr["o"] += 1
                nk = len(ktl)
                npair = (nk + 1) // 2

                AHEAD = 4
                bslot = {}

                def issue_bias(j):
                    if j < nk and ktl[j][2] is not None and ktl[j][2][0] == "dram":
                        i = ctr["nb"] % NNB
                        ctr["nb"] += 1
                        S.dma("pool", nbt[i][:, 0:nq], ktl[j][2][1], w=[b_nbt[i]])
                        bslot[j] = i
                for jj_ in range(AHEAD):
                    issue_bias(jj_)

                def s_pair(i):
                    pk = ctr["s"] % 2
                    ctr["s"] += 1
                    js = [j for j in (2 * i, 2 * i + 1) if j < nk]
                    rr = [b_k, b_q]
                    items = []
                    for c, j in enumerate(js):
                        kap, vap, bias = ktl[j]
                        issue_bias(j + AHEAD)
                        bsb = None
                        if bias is not None:
                            if bias[0] == "dram":
                                bi_ = bslot[j]
                                bsb = nbt[bi_][:, 0:nq]
                                rr = rr + [b_nbt[bi_], b_cst]
                            else:
                                bsb = bias[1]
                                rr = rr + [b_dm, b_cst]
                        items.append((c, kap, bsb))

                    def g():
                        ins = None
                        for (c, kap, bsb) in items:
                            ins = nc.tensor.matmul(bank2[pk][:, c, 0:nq], lhsT=kap, rhs=qap, start=True, stop=(bsb is None))
                            if bsb is not None:
                                ins = nc.tensor.matmul(bank2[pk][:, c, 0:nq], lhsT=ident, rhs=bsb, start=False, stop=True)
                        return ins
                    MMG(g, r=rr, w=[b_pair[pk]])
                    return pk, js

                def emit_den(i_, drhs_, rbuf_):
                    MMG(lambda: nc.tensor.matmul(banks[db][:, 0:nq], lhsT=ones, rhs=drhs_, start=(i_ == 0), stop=(i_ == npair - 1)),
                        r=[rbuf_, b_cst], w=[b_bank[db]])
                pend = None
                nxt = s_pair(0)
                for i in range(npair):
                    cur, js = nxt
                    if i + 1 < npair:
                        nxt = s_pair(i + 1)
                    pi = ctr["p"] % NPT
                    ctr["p"] += 1
                    nt_ = len(js)
                    ACT(pt2[pi][:, 0:nt_, 0:nq], bank2[cur][:, 0:nt_, 0:nq], AF.Exp, r=[b_pair[cur]], w=[b_pt[pi]], scale=scale)

                    def g2():
                        ins = None
                        for c, j in enumerate(js):
                            ins = nc.tensor.matmul(banks[ob][:, 0:nq], lhsT=ktl[j][1], rhs=pt2[pi][:, c, 0:nq], start=(j == 0), stop=(j == nk - 1))
                        return ins
                    MMG(g2, r=[b_v, b_pt[pi]], w=[b_bank[ob]])
                    if nt_ == 2:
                        TT("dve", pd[pi][:, 0:nq], pt2[pi][:, 0, 0:nq], pt2[pi][:, 1, 0:nq], ALU.add, r=[b_pt[pi]], w=[b_pd[pi]])
                        drhs, rbuf = pd[pi][:, 0:nq], b_pd[pi]
                    else:
                        drhs, rbuf = pt2[pi][:, 0, 0:nq], b_pt[pi]
                    if pend is not None:
                        emit_den(*pend)
                    pend = (i, drhs, rbuf)
                emit_den(*pend)
                return ob, db

            def recip(db, nq, sink_ap=None):
                ri = ctr["r"] % 2
                ctr["r"] += 1
                if sink_ap is not None:
                    TS(rec[ri][:, 0:nq], banks[db][:, 0:nq], sink_ap, None, ALU.add, None, r=[b_bank[db], b_derl[l]], w=[b_rec[ri]])
                    S.op("dve", lambda: nc.vector.reciprocal(out=rec[ri][:, 0:nq], in_=rec[ri][:, 0:nq]), r=[b_rec[ri]], w=[b_rec[ri]])
                else:
                    S.op("dve", lambda: nc.vector.reciprocal(out=rec[ri][:, 0:nq], in_=banks[db][:, 0:nq]), r=[b_bank[db]], w=[b_rec[ri]])
                return ri

            qchunks = [(t0, n, 0) for (t0, n) in chunks(0, SX, 512)]
            if with_ctx:
                qchunks.append((SX, SC, 1))
            for m in range(4):
                qb0 = 4 * m
                kb0, nkb = [(0, 2), (2, 4), (6, 4), (10, 2)][m]
                vc0, vw = [(0, 256), (256, 512), (768, 512), (1280, 256)][m]
                S.dma("sp", qs[:, :, :], qT_d[qb0:qb0 + 4].rearrange("k p t -> p k t"), w=[b_q])
                S.dma("sp", ks[:, 0:nkb, :], kT_d[kb0:kb0 + nkb].rearrange("k p t -> p k t"), w=[b_k])
                S.dma("sp", vs[:, :, 0:vw], v_d[:, :, vc0:vc0 + vw].rearrange("t p c -> p t c"), w=[b_v])
                for (q0, nq, qcls) in qchunks:
                    qi = q0 // 512
                    if m != 2:
                        for h in range(4):
                            if hook is not None and m == 0:
                                next(hook, None)
                            if m in (0, 3):
                                kvh = h // 2
                            else:
                                kvh = h
                            if qcls == 1:
                                tiles = [(16, None), (17, None)]
                            elif m == 0:
                                tiles = [(j, None) for j in range(18)]
                            elif m == 1:
                                tiles = [(j, ("dram", nbias_d[l, h, NB_OFF[qi] + jj])) for jj, j in enumerate(NB_TILES[qi])]
                                tiles += [(16, None), (17, None)]
                            else:
                                tiles = []
                                for jj in range(6):
                                    j = 4 * qi - 1 + jj
                                    if 0 <= j < 16:
                                        tiles.append((j, ("sb", dm[:, jj, :])))
                                tiles += [(16, None), (17, None)]
                            ktl = [(ks[:, kvh, j * 128:(j + 1) * 128], vs[:, j, kvh * 128:(kvh + 1) * 128], b) for (j, b) in tiles]
                            scale = 1.0
                            ob, db = attend(qs[:, h, q0:q0 + nq], ktl, scale)
                            ri = recip(db, nq, der[:, l, 66 + h:67 + h] if m == 3 else None)
                            TT("dve", ost[:, h, q0:q0 + nq], banks[ob][:, 0:nq], rec[ri][:, 0:nq], ALU.mult,
                               r=[b_bank[ob], b_rec[ri]], w=[b_ost])
                    else:
                        for h in range(4):
                            if hook is not None:
                                next(hook, None)
                                next(hook, None)
                            tiles = [16, 17] if qcls == 1 else list(range(18))
                            for sub in range(2):
                                p0 = 64 * sub
                                ktl = [(ks[p0:p0 + 64, h, j * 128:(j + 1) * 128], vs[:, j, h * 128:(h + 1) * 128], None) for j in tiles]
                                ob, db = attend(qs[p0:p0 + 64, h, q0:q0 + nq], ktl, 64.0 ** -0.5)
                                ri = recip(db, nq)
                                dst, bdst = (o1, b_o1) if sub == 0 else (o2, b_o2)
                                TT("dve", dst[:, 0:nq], banks[ob][:, 0:nq], rec[ri][:, 0:nq], ALU.mult,
                                   r=[b_bank[ob], b_rec[ri]], w=[bdst])
                            STT(o1[:, 0:nq], o2[:, 0:nq], der[:, l, 70:71], o1[:, 0:nq], ALU.mult, ALU.add, r=[b_o1, b_o2, b_derl[l]], w=[b_o1])
                            ACT(sqb[:, 0:nq], o1[:, 0:nq], AF.Square, r=[b_o1], w=[b_sqb])
                            MMG(lambda: nc.tensor.matmul(banks[N_BANK][:, 0:nq], lhsT=ones, rhs=sqb[:, 0:nq], start=True, stop=True),
                                r=[b_sqb, b_cst], w=[b_bank[N_BANK]])
                            ACT(rstd[:, 0:nq], banks[N_BANK][:, 0:nq], AF.Ln, r=[b_bank[N_BANK]], w=[b_rstd], scale=1.0 / 128, bias=EPS)
                            ACT(rstd[:, 0:nq], rstd[:, 0:nq], AF.Exp, r=[b_rstd], w=[b_rstd], scale=-0.5)
                            STT(ost[:, h, q0:q0 + nq], o1[:, 0:nq], der[:, l, 71:72], rstd[:, 0:nq], ALU.mult, ALU.mult,
                                r=[b_o1, b_rstd, b_derl[l]], w=[b_ost])
                if hook is not None and m == 3:
                    for _ in hook:
                        pass
                tq = NT if with_ctx else SX
                S.dma("sp", oT_d[qb0:qb0 + 4].rearrange("k p t -> p k t")[:, :, 0:tq], ost[:, :, 0:tq], r=[b_ost])
                S.barrier()

    def phase_out(l, xsrc, with_ctx):
        with ExitStack() as ps:
            wo = sb("wo", [128, KC, D], BF16, ps)
            b_wo = Buf()
            for i in range(4):
                S.dma("pool", wo[:, :, i * 512:(i + 1) * 512], wout_d[l].rearrange("(k p) n -> p k n", p=128)[:, :, i * 512:(i + 1) * 512], w=[b_wo])
            oc = [sb("oc%d" % i, [128, KC, 256], BF16, ps) for i in range(2)]
            b_oc = [Buf(), Buf()]
            xc = [sb("xc%d" % i, [128, KC, 256], F32, ps) for i in range(2)]
            b_xc = [Buf(), Buf()]
            sq = sb("sq", [128, KC, 256], BF16, ps)
            b_sq = Buf()
            rstd = sb("rstd", [128, 256], F32, ps)
            b_rstd = Buf()
            hst = [sb("hst%d" % i, [128, KC, 256], BF16, ps) for i in range(2)]
            b_hst = [Buf(), Buf()]
            b_hst2 = [Buf(), Buf()]
            tmpb = sb("tmpb", [128, KC, 256], F32, ps)
            b_tmpb = [Buf() for _ in range(KC)]
            gi = 0
            chs = tok_chunks(NT if with_ctx else SX, 256)
            b_xcn = [[Buf() for _ in range(KC)] for _ in range(2)]
            b_sqn = [Buf() for _ in range(KC)]

            def loads(ci):
                t0, n, cls = chs[ci]
                S.dma("sp", oc[ci % 2][:, :, 0:n], oT_d.rearrange("k p t -> p k t")[:, :, t0:t0 + n], w=[b_oc[ci % 2]])
                S.dma("sp", xc[ci % 2][:, :, 0:n], xsrc.rearrange("k p t -> p k t")[:, :, t0:t0 + n], w=b_xcn[ci % 2])
            loads(0)
            for ci, (t0, n, cls) in enumerate(chs):
                if ci + 1 < len(chs):
                    loads(ci + 1)
                o_c, bo = oc[ci % 2], b_oc[ci % 2]
                x_c, bxs = xc[ci % 2], b_xcn[ci % 2]
                h_s, bh = hst[ci % 2], b_hst[ci % 2]

                def ones_mm(k_):
                    MMG(lambda: nc.tensor.matmul(banks[6][:, 0:n], lhsT=ones, rhs=sq[:, k_, 0:n], start=(k_ == 0), stop=(k_ == KC - 1)),
                        r=[b_sqn[k_], b_cst], w=[b_bank[6]])
                for nb in range(KC):
                    bk = gi % 4
                    gi += 1
                    pb = banks[bk]

                    def grp():
                        ins = None
                        for kc in range(KC):
                            ins = nc.tensor.matmul(pb[:, 0:n], lhsT=wo[:, kc, nb * 128:(nb + 1) * 128], rhs=o_c[:, kc, 0:n],
                                                   start=(kc == 0), stop=(kc == KC - 1))
                        return ins
                    MMG(grp, r=[b_wo, bo], w=[b_bank[bk]])
                    STT(x_c[:, nb, 0:n], pb[:, 0:n], MOD(l, 2, nb, cls), x_c[:, nb, 0:n], ALU.mult, ALU.add,
                        r=[b_bank[bk], b_modl[l], bxs[nb]], w=[bxs[nb]])
                    ACT(sq[:, nb, 0:n], x_c[:, nb, 0:n], AF.Square, r=[bxs[nb]], w=[b_sqn[nb]])
                for nb in range(KC):
                    ones_mm(nb)
                S.dma("sp", xres.rearrange("k p t -> p k t")[:, :, t0:t0 + n], x_c[:, :, 0:n], r=bxs)
                ACT(rstd[:, 0:n], banks[6][:, 0:n], AF.Ln, r=[b_bank[6]], w=[b_rstd], scale=1.0 / D, bias=EPS)
                ACT(rstd[:, 0:n], rstd[:, 0:n], AF.Exp, r=[b_rstd], w=[b_rstd], scale=-0.5)
                for kc in range(KC):
                    t = tmpb[:, kc, :]
                    bt = b_tmpb[kc]
                    TT("dve", t[:, 0:n], x_c[:, kc, 0:n], rstd[:, 0:n], ALU.mult, r=[bxs[kc], b_rstd], w=[bt])
                    if kc % 2 == 1:
                        TS(h_s[:, kc, 0:n], t[:, 0:n], A2(l, kc, cls), MOD(l, 3, kc, cls), ALU.mult, ALU.add,
                           r=[bt, b_derl[l], b_modl[l]], w=[b_hst2[ci % 2]], eng="pool")
                    else:
                        ACT(h_s[:, kc, 0:n], t[:, 0:n], AF.Identity, r=[bt, b_derl[l], b_modl[l]], w=[bh],
                            scale=A2(l, kc, cls), bias=MOD(l, 3, kc, cls))
                S.dma("sp", hT_d.rearrange("k p t -> p k t")[:, :, t0:t0 + n], h_s[:, :, 0:n], r=[bh, b_hst2[ci % 2]])
            S.barrier()

    def phase_mlp(l, with_ctx, last):
        with ExitStack() as ps:
            TSUP = 1152 if with_ctx else 1024
            acc = sb("acc", [128, KC, TSUP], F32, ps)
            b_acc = Buf()
            h2 = sb("h2", [128, KC, TSUP], BF16, ps)
            b_h2 = Buf()
            ug = sb("ug", [128, 8, TSUP], BF16, ps)
            b_ug = Buf()
            rl = [sb("rl%d" % i, [128, 512], F32, ps) for i in range(2)]
            b_rl = [Buf(), Buf()]
            if last:
                sq = sb("fsq", [128, KC, 128], BF16, ps)
                b_sq = Buf()
                rstd = sb("frstd", [128, 512], F32, ps)
                b_rstd = Buf()
                tmp = [sb("ftmp%d" % i, [128, 512], F32, ps) for i in range(4)]
                b_tmp = [Buf() for _ in range(4)]
                fo = [sb("fo%d" % i, [128, KC, 128], F32, ps) for i in range(2)]
                b_fo = [Buf(), Buf()]
            gi = 0
            ri = 0
            for s in range(2):
                ts0 = s * TSUP
                ts1 = ts0 + TSUP
                subs = [(t0, n, cls) for (t0, n, cls) in tok_chunks(NT, 512) if False]
                subs = []
                for (a, b_, cls) in ((ts0, min(ts1, SX), 0), (max(ts0, SX), ts1, 1)):
                    if b_ > a:
                        subs += [(t, n, cls) for (t, n) in chunks(a, b_, 512)]
                S.dma("sp", acc[:, :, 0:TSUP], xres.rearrange("k p t -> p k t")[:, :, ts0:ts1], w=[b_acc])
                if s == 0:
                    S.dma("sp", h2[:, :, 0:TSUP], hT_d.rearrange("k p t -> p k t")[:, :, ts0:ts1], w=[b_h2])
                loads = []
                for g in range(8):
                    for i in range(2):
                        loads.append(("up", wup_d[l].rearrange("(k p) n -> p k n", p=128)[:, :, g * 1024 + i * 512: g * 1024 + (i + 1) * 512]))
                    for i in range(2):
                        loads.append(("dn", wdown_d[l][g * 1024:(g + 1) * 1024, i * 1024:(i + 1) * 1024].rearrange("(k p) n -> p k n", p=128)))

                class MixStream(WStream):
                    def _issue(self2, i):
                        s_ = ring_ctr[0] % 3
                        ring_ctr[0] += 1
                        kind, ap = self2.loads[i]
                        a = 16 if kind == "up" else 8
                        dst = ring[s_][:, :].rearrange("p (a b) -> p a b", a=a)
                        S.dma("pool", dst, ap, w=[b_ring[s_]])
                        self2.slots[i] = s_
                ws = MixStream(loads, 16)
                for g in range(8):
                    for i in range(2):
                        wt, bw = ws.get(g * 4 + i)
                        for j in range(4):
                            hc = i * 4 + j
                            for (t0, n, cls) in subs:
                                bk = gi % 4
                                gi += 1
                                pb = banks[bk]
                                lo = t0 - ts0

                                def grp():
                                    ins = None
                                    for kc in range(KC):
                                        ins = nc.tensor.matmul(pb[:, 0:n], lhsT=wt[:, kc * 512 + j * 128: kc * 512 + (j + 1) * 128],
                                                               rhs=h2[:, kc, lo:lo + n], start=(kc == 0), stop=(kc == KC - 1))
                                    return ins
                                MMG(grp, r=[bw, b_h2], w=[b_bank[bk]])
                                r_ = rl[ri % 2]
                                br = b_rl[ri % 2]
                                ri += 1
                                ACT(r_[:, 0:n], pb[:, 0:n], AF.Relu, r=[b_bank[bk]], w=[br])
                                TT("pool", ug[:, hc, lo:lo + n], r_[:, 0:n], r_[:, 0:n], ALU.mult, r=[br], w=[b_ug])
                    if g == 7 and s == 0:
                        S.dma("sp", h2[:, :, 0:TSUP], hT_d.rearrange("k p t -> p k t")[:, :, TSUP:2 * TSUP], w=[b_h2])
                    for i in range(2):
                        wt, bw = ws.get(g * 4 + 2 + i)
                        for nbl in range(8):
                            nb = i * 8 + nbl
                            for (t0, n, cls) in subs:
                                bk = 4 + gi % 4
                                gi += 1
                                pb = banks[bk]
                                lo = t0 - ts0

                                def grpd():
                                    ins = None
                                    for hc in range(8):
                                        ins = nc.tensor.matmul(pb[:, 0:n], lhsT=wt[:, hc * 1024 + nbl * 128: hc * 1024 + (nbl + 1) * 128],
                                                               rhs=ug[:, hc, lo:lo + n], start=(hc == 0), stop=(hc == 7))
                                    return ins
                                MMG(grpd, r=[bw, b_ug], w=[b_bank[bk]])
                                STT(acc[:, nb, lo:lo + n], pb[:, 0:n], MOD(l, 5, nb, cls), acc[:, nb, lo:lo + n], ALU.mult, ALU.add,
                                    r=[b_bank[bk], b_modl[l], b_acc], w=[b_acc])
                if not last:
                    S.dma("sp", xres.rearrange("k p t -> p k t")[:, :, ts0:ts1], acc[:, :, 0:TSUP], r=[b_acc])
                else:
                    go, _ = _SM["gfin"]
                    for ci, (t0, n) in enumerate(chunks(0, TSUP, 128)):
                        f_o, bf = fo[ci % 2], b_fo[ci % 2]
                        norm_mod(acc[:, :, t0:t0 + n], b_acc, n, sq, b_sq, rstd, b_rstd, tmp, b_tmp, 3,
                                 lambda kc: small[:, go + kc: go + kc + 1], None,
                                 lambda kc: f_o[:, kc, 0:n], bf, final=True)
                        S.dma("sp", out_d.rearrange("k p t -> p k t")[:, :, ts0 + t0: ts0 + t0 + n], f_o[:, :, 0:n], r=[bf])
            S.barrier()

    for _ in mod_steps(0, 0, 8):
        pass
    der_a(0)
    S.barrier()
    dsteps = deferred_steps()
    stages = []
    for l in range(L):
        with_ctx = l < L - 1
        xsrc = xin if l == 0 else xres
        stages.append(("in%d" % l, lambda l=l, xsrc=xsrc: phase_in(l, xsrc)))
        stages.append(("attn%d" % l, lambda l=l, wc=with_ctx: phase_attn(l, wc, dsteps if l == 0 else None)))
        stages.append(("out%d" % l, lambda l=l, xsrc=xsrc, wc=with_ctx: phase_out(l, xsrc, wc)))
        stages.append(("mlp%d" % l, lambda l=l, wc=with_ctx: phase_mlp(l, wc, l == L - 1)))
    for name, fn in stages:
        fn()
        if stop_after == name:
            break
    S.barrier()
    es.close()
    return nc


def _pvec(v):
    return np.ascontiguousarray(np.asarray(v, np.float32).reshape(KC, 128).T)


def _rope_tables():
    def axial(n_tok, dim):
        t = np.arange(n_tok)
        row = (t // 64).astype(np.float32)
        col = (t % 64).astype(np.float32)
        nf = dim // 4
        inv = (np.float32(10000.0) ** (-np.arange(nf, dtype=np.float32) / np.float32(nf))).astype(np.float32)
        ang = np.concatenate([row[:, None] * inv, col[:, None] * inv], axis=-1).astype(np.float32)
        return np.cos(ang).astype(np.float32), np.sin(ang).astype(np.float32)
    ch, sh = axial(SX, 128)
    cd, sd = axial(SX, 64)
    tab = np.zeros((128, 4, NT), np.float32)
    tab[:, 0, SX:] = 1.0
    tab[:, 2, SX:] = 1.0
    p = np.arange(128)
    tab[:, 0, :SX] = ch[:, p % 64].T
    sgn = np.where(p < 64, 1.0, -1.0).astype(np.float32)
    tab[:, 1, :SX] = sh[:, p % 64].T * sgn[:, None]
    tab[:, 2, :SX] = cd[:, p % 32].T
    sgn2 = np.where((p // 32) % 2 == 0, 1.0, -1.0).astype(np.float32)
    tab[:, 3, :SX] = sd[:, p % 32].T * sgn2[:, None]
    return tab


def _perm_cols():
    perm = np.arange(PROJ)
    h128 = np.concatenate([np.arange(0, 128, 2), np.arange(1, 128, 2)])
    h64 = np.concatenate([np.arange(0, 64, 2), np.arange(1, 64, 2)])
    for base, nh in ((0, 4), (512, 2), (4096, 4), (4608, 2)):
        for h in range(nh):
            b = base + h * 128
            perm[b:b + 128] = b + h128
    for base in (2560, 3072):
        for h in range(8):
            b = base + h * 64
            perm[b:b + 64] = b + h64
    return perm, h128


def _nb_bias(rpb):
    out = np.empty((4, NB_NT, 128, 512), np.float32)
    for qi in range(4):
        tq = 512 * qi + np.arange(512)
        qr, qc = tq // 64, tq % 64
        rs = np.clip(qr - 4, 0, 32 - 8)
        cs = np.clip(qc - 8, 0, 64 - 16)
        for jj, j in enumerate(NB_TILES[qi]):
            tk = 128 * j + np.arange(128)
            kr, kcol = tk // 64, tk % 64
            valid = ((kr[:, None] >= rs[None, :]) & (kr[:, None] < rs[None, :] + 8) &
                     (kcol[:, None] >= cs[None, :]) & (kcol[:, None] < cs[None, :] + 16))
            dr = np.clip(kr[:, None] - qr[None, :] + 7, 0, 14)
            dc = np.clip(kcol[:, None] - qc[None, :] + 15, 0, 30)
            for h in range(4):
                out[h, NB_OFF[qi] + jj] = np.where(valid, rpb[h][dr, dc], np.float32(NEG))
    return out


def _dmask():
    m = np.empty((128, 6, 512), np.float32)
    kk = np.arange(128)[:, None]
    qq = np.arange(512)[None, :]
    for jj in range(6):
        diff = (-128 + 128 * jj + kk) - qq
        m[:, jj, :] = np.where(np.abs(diff) <= 128, 0.0, NEG)
    return m


def prep_inputs(inp):
    f = lambda a: np.asarray(a, np.float32)
    x, c, ctx, c_ctx = f(inp["x"]), f(inp["c"]), f(inp["ctx"]), f(inp["c_ctx"])
    perm, h128 = _perm_cols()
    shared = {
        "rope": _rope_tables(),
        "consts": np.concatenate([np.eye(128, dtype=np.float32), np.ones((128, 128), np.float32)], axis=1),
        "dmask": _dmask(),
        "wmod": np.ascontiguousarray(f(inp["w_mod"])),
        "win": np.ascontiguousarray(f(inp["w_in"])[:, :, perm]),
        "wout": np.ascontiguousarray(f(inp["w_out"])),
        "wup": np.ascontiguousarray(f(inp["w_up"])),
        "wdown": np.ascontiguousarray(f(inp["w_down"])),
        "nbias": np.stack([_nb_bias(f(inp["na_rpb"])[l]) for l in range(L)]),
    }
    small0 = np.zeros((128, NS), np.float32)

    def put(name, arr):
        o, w = _SM[name]
        small0[:, o:o + w] = arr.reshape(128, w)
    for l in range(L):
        put("gmix%d" % l, _pvec(inp["g_mix"][l]))
        put("gmlp%d" % l, _pvec(inp["g_mlp"][l]))
        put("bmod%d" % l, np.ascontiguousarray(f(inp["b_mod"])[l].reshape(96, 128).T))
        put("gq%d" % l, f(inp["gqa_gq"])[l][h128][:, None])
        put("gk%d" % l, f(inp["gqa_gk"])[l][h128][:, None])
        put("gsub%d" % l, f(inp["diff_gsub"])[l][:, None])
        for n, key in (("lq1", "diff_lq1"), ("lk1", "diff_lk1"), ("lq2", "diff_lq2"), ("lk2", "diff_lk2")):
            put("%s%d" % (n, l), np.broadcast_to(f(inp[key])[l][None, :], (128, 64)).copy())
        put("sink%d" % l, np.broadcast_to(f(inp["swa_sink"])[l][None, :], (128, 4)).copy())
    put("gfin", _pvec(inp["g_final"]))
    in_maps = []
    for b in range(8):
        xt = np.concatenate([x[b], ctx[b]], axis=0)
        xin = np.ascontiguousarray(xt.T.reshape(KC, 128, NT))
        sm = small0.copy()
        o, w = _SM["cond"]
        cond = np.stack([_pvec(c[b]), _pvec(c_ctx)], axis=-1)
        sm[:, o:o + w] = cond.reshape(128, 32)
        d = dict(shared)
        d["xin"] = xin
        d["small"] = sm
        in_maps.append(d)
    return in_maps


_NC_CACHE = {}


def kernel(**inputs):
    in_maps = prep_inputs(inputs)
    if "nc" not in _NC_CACHE:
        _NC_CACHE["nc"] = build()
    nc = _NC_CACHE["nc"]
    res = run_bass_kernel_spmd(nc, in_maps, core_ids=list(range(8)))
    outs = []
    for b in range(8):
        o = np.asarray(res.results[b]["out"], np.float32)
        outs.append(o.reshape(D, SX).T)
    return np.ascontiguousarray(np.stack(outs, axis=0))
```
